# Optimizing a Trainium2 kernel written in Bass

```python
import jax, jax.numpy as jnp
from jax import lax
import numpy as np

D_MODEL = 1024
BATCH = 8
SEQ = 4096
DEPTH = 1

CHUNK = 64
SGU_BLOCK = 128
SGU_WIDTH = 1024
SGU_GROUPS = 8
SGU_GROUP_DIM = SGU_WIDTH // SGU_GROUPS
HGRN_WIDTH = 1024
HGRN_EXPAND = 128
HGRN_HEADS = HGRN_WIDTH // HGRN_EXPAND
N_BRANCH = 2
D_FF = 2816
CONV_WIDTH = 3
PLE_DIM = 256
LN_EPS = 1e-5
RMS_EPS = 1e-6
ALPHA = (2 * DEPTH) ** 0.25
BETA = (8 * DEPTH) ** -0.25
IN_COLS = 2 * SGU_WIDTH + 4 * HGRN_WIDTH + N_BRANCH * D_MODEL

kernel_name = "chunk_causal_sgu_hgrn2_hybrid"


def layer_norm(x, g, b):
    xf = x.astype(jnp.float32)
    mu = jnp.mean(xf, axis=-1, keepdims=True)
    var = jnp.mean(jnp.square(xf - mu), axis=-1, keepdims=True)
    return ((xf - mu) * lax.rsqrt(var + LN_EPS)).astype(x.dtype) * g + b


def sgu_mixer(u, v, w_s, b_s, g_v, b_v):
    bsz, t_len, _ = u.shape
    n_blk = t_len // SGU_BLOCK
    v = layer_norm(v, g_v, b_v).reshape(bsz, n_blk, SGU_BLOCK, SGU_GROUPS, SGU_GROUP_DIM)
    chunk_id = jnp.arange(SGU_BLOCK) // CHUNK
    mask = chunk_id[:, None] >= chunk_id[None, :]
    w = jnp.where(mask[None], w_s, jnp.zeros((), w_s.dtype))
    mixed = jnp.einsum('gts,bnsgc->bntgc', w, v) + b_s.T[None, None, :, :, None]
    return u * mixed.reshape(bsz, t_len, SGU_WIDTH)


def hgrn2_mixer(q, f_pre, i_in, og, lb, g_norm):
    bsz, t_len, _ = q.shape
    n_chunk = t_len // CHUNK
    f32 = jnp.float32
    qf = jax.nn.silu(q.astype(f32))
    f = lb + (1.0 - lb) * jax.nn.sigmoid(f_pre.astype(f32))
    kf = 1.0 - f
    logf = jnp.log(f)

    def to_chunks(z):
        return z.reshape(bsz, n_chunk, CHUNK, HGRN_HEADS, HGRN_EXPAND).transpose(1, 0, 3, 2, 4)

    qc, kc, ic = to_chunks(qf), to_chunks(kf), to_chunks(i_in.astype(f32))
    cc = jnp.cumsum(to_chunks(logf), axis=3)
    tri = jnp.tril(jnp.ones((CHUNK, CHUNK), dtype=bool))

    def step(state, inp):
        qk, kk, ik, ck = inp
        diff = ck[:, :, :, None, :] - ck[:, :, None, :, :]
        decay = jnp.exp(jnp.where(tri[None, None, :, :, None], diff, -jnp.inf))
        attn = jnp.einsum('bhte,bhse,bhtse->bhts', qk, kk, decay)
        o = (jnp.einsum('bhts,bhsv->bhtv', attn, ik)
             + jnp.einsum('bhte,bhev->bhtv', qk * jnp.exp(ck), state))
        c_last = ck[:, :, -1:, :]
        state = (jnp.exp(c_last[:, :, 0, :])[..., None] * state
                 + jnp.einsum('bhse,bhsv->bhev', kk * jnp.exp(c_last - ck), ik))
        return state, o

    s0 = jnp.zeros((bsz, HGRN_HEADS, HGRN_EXPAND, HGRN_EXPAND), f32)
    _, o = lax.scan(step, s0, (qc, kc, ic, cc))
    o = o.transpose(1, 0, 3, 2, 4).reshape(bsz, t_len, HGRN_HEADS, HGRN_EXPAND)
    o = o * lax.rsqrt(jnp.mean(jnp.square(o), axis=-1, keepdims=True) + RMS_EPS)
    o = o.reshape(bsz, t_len, HGRN_WIDTH).astype(q.dtype) * g_norm
    return o * jax.nn.silu(og)


def causal_dwconv(x, w, b):
    y = lax.conv_general_dilated(
        x, w[:, None, :], window_strides=(1,), padding=[(CONV_WIDTH - 1, 0)],
        dimension_numbers=('NWC', 'WIO', 'NWC'), feature_group_count=x.shape[-1])
    return y + b


def conv_ffn(x, w_up, conv_w, conv_b, w_down):
    h = x @ w_up
    gate, val = h[..., :D_FF], h[..., D_FF:]
    gate = causal_dwconv(gate, conv_w, conv_b)
    return (jax.nn.gelu(gate) * val) @ w_down


def setup_inputs(seed: int = 0) -> dict:
    key = jax.random.key(seed)
    ks = jax.random.split(key, 24)
    n = lambda k, s, sc: jax.random.normal(k, s, jnp.float32) * sc
    L, D = DEPTH, D_MODEL
    return {
        "x": n(ks[0], (BATCH, SEQ, D), 1.0),
        "p": n(ks[1], (L, BATCH, SEQ, PLE_DIM), 1.0),
        "w_in": n(ks[2], (L, D, IN_COLS), D ** -0.5),
        "sgu_w_s": n(ks[3], (L, SGU_GROUPS, SGU_BLOCK, SGU_BLOCK), SGU_BLOCK ** -0.5),
        "sgu_b_s": 1.0 + n(ks[4], (L, SGU_GROUPS, SGU_BLOCK), 0.02),
        "sgu_norm_g": 1.0 + n(ks[5], (L, SGU_WIDTH), 0.02),
        "sgu_norm_b": n(ks[6], (L, SGU_WIDTH), 0.02),
        "hgrn_lb_logits": n(ks[7], (L + 1, HGRN_WIDTH), 0.1),
        "hgrn_norm_g": 1.0 + n(ks[8], (L, HGRN_WIDTH), 0.02),
        "w_branch": n(ks[9], (L, N_BRANCH, SGU_WIDTH, D), SGU_WIDTH ** -0.5),
        "w_out": n(ks[10], (L, D, D), D ** -0.5 * BETA),
        "ln1_g": 1.0 + n(ks[11], (L, D), 0.02),
        "ln1_b": n(ks[12], (L, D), 0.02),
        "ffn_w_up": n(ks[13], (L, D, 2 * D_FF), D ** -0.5),
        "ffn_conv_w": n(ks[14], (L, CONV_WIDTH, D_FF), CONV_WIDTH ** -0.5),
        "ffn_conv_b": n(ks[15], (L, D_FF), 0.02),
        "ffn_w_down": n(ks[16], (L, D_FF, D), D_FF ** -0.5 * BETA),
        "ln2_g": 1.0 + n(ks[17], (L, D), 0.02),
        "ln2_b": n(ks[18], (L, D), 0.02),
        "ple_w_proj": n(ks[19], (L, PLE_DIM, D), PLE_DIM ** -0.5 * BETA),
        "ple_w_gate": n(ks[20], (L, D, D), D ** -0.5),
    }


def reference(x, p, w_in, sgu_w_s, sgu_b_s, sgu_norm_g, sgu_norm_b, hgrn_lb_logits,
              hgrn_norm_g, w_branch, w_out, ln1_g, ln1_b, ffn_w_up, ffn_conv_w,
              ffn_conv_b, ffn_w_down, ln2_g, ln2_b, ple_w_proj, ple_w_gate):
    splits = [SGU_WIDTH, 2 * SGU_WIDTH,
              2 * SGU_WIDTH + HGRN_WIDTH, 2 * SGU_WIDTH + 2 * HGRN_WIDTH,
              2 * SGU_WIDTH + 3 * HGRN_WIDTH, 2 * SGU_WIDTH + 4 * HGRN_WIDTH,
              2 * SGU_WIDTH + 4 * HGRN_WIDTH + D_MODEL]
    lb_all = jnp.cumsum(jax.nn.softmax(hgrn_lb_logits.astype(jnp.float32), axis=0), axis=0)
    for l in range(DEPTH):
        h = x @ w_in[l]
        u, v, q, f_pre, i_in, og, g_a, g_b = jnp.split(h, splits, axis=-1)
        y_a = sgu_mixer(jax.nn.gelu(u), jax.nn.gelu(v), sgu_w_s[l], sgu_b_s[l],
                        sgu_norm_g[l], sgu_norm_b[l])
        y_b = hgrn2_mixer(q, f_pre, i_in, og, lb_all[l], hgrn_norm_g[l])
        merged = (jax.nn.sigmoid(g_a) * (y_a @ w_branch[l, 0])
                  + jax.nn.sigmoid(g_b) * (y_b @ w_branch[l, 1]))
        x = layer_norm(ALPHA * x + merged @ w_out[l], ln1_g[l], ln1_b[l])
        ffn = conv_ffn(x, ffn_w_up[l], ffn_conv_w[l], ffn_conv_b[l], ffn_w_down[l])
        ple = jax.nn.sigmoid(x @ ple_w_gate[l]) * (p[l] @ ple_w_proj[l])
        x = layer_norm(ALPHA * x + ffn + ple, ln2_g[l], ln2_b[l])
    return x
```

```python
import numpy as np
from contextlib import ExitStack
import concourse.bass as bass
import concourse.mybir as mybir
from concourse.bass_utils import run_bass_kernel_spmd

F32 = mybir.dt.float32
BF16 = mybir.dt.bfloat16
AF = mybir.ActivationFunctionType
ALU = mybir.AluOpType

D = 1024
KC = 8
TT = 512
DFF = 2816
FC = 22
PLE = 256
NCORES = 8
ALPHA = float(2.0 ** 0.25)
LN_EPS = 1e-5
RMS_EPS = 1e-6
SLOT_CAP = 2816
NSLOT = 5

_CST = {}
_o = 0
for _n, _w in (("ln1g", 8), ("ln1b", 8), ("ln2g", 8), ("ln2b", 8), ("hg", 8), ("l0", 8), ("l1", 8),
               ("cw", 66), ("cb", 22), ("gv", 1024), ("bv", 1024)):
    _CST[_n] = _o
    _o += _w
NCST = _o


class Buf:
    __slots__ = ("name", "w", "r", "excl")

    def __init__(self, name, excl=False):
        self.name = name
        self.w = None
        self.r = {}
        self.excl = excl


class FW:
    ENG = ("pe", "act", "dve", "pool", "sp")

    def __init__(self, nc, es):
        self.nc = nc
        self.prog = {e: [] for e in self.ENG}
        self.esem = {e: es.enter_context(nc.semaphore("es_" + e)) for e in ("pe", "act", "dve", "pool")}
        self.cnt = {e: 0 for e in self.ENG}
        self.seen = {e: {} for e in self.ENG}
        self.dcount = {}
        self.nwait = 0

    def _waits(self, eng, reads, writes):
        need = {}

        def add(tok):
            if tok is None:
                return
            sem, val = tok
            k = id(sem)
            if k not in need or need[k][1] < val:
                need[k] = (sem, val)

        for b in reads:
            add(b.w)
            if b.excl:
                for tok in b.r.values():
                    add(tok)
        for b in writes:
            add(b.w)
            for tok in b.r.values():
                add(tok)
        out = []
        for k, (sem, val) in need.items():
            if eng == "pe" and sem is self.esem["pe"]:
                continue
            if self.seen[eng].get(k, 0) >= val:
                continue
            self.seen[eng][k] = val
            out.append((sem, val))
        return out

    def _mark(self, reads, writes, tok):
        k = id(tok[0])
        for b in reads:
            b.r[k] = tok
        for b in writes:
            b.w = tok
            b.r = {}

    def _emit_waits(self, eng, reads, writes):
        for sem, val in self._waits(eng, reads, writes):
            self.nwait += 1
            self.prog[eng].append(lambda e, sem=sem, val=val: e.wait_ge(sem, val))

    def op(self, eng, reads, writes, fn):
        self._emit_waits(eng, reads, writes)
        self.cnt[eng] += 1
        sem = self.esem[eng]
        tok = (sem, self.cnt[eng])
        self.prog[eng].append(lambda e, fn=fn, sem=sem: fn(e).then_inc(sem, 1))
        self._mark(reads, writes, tok)

    def dma(self, q, dsem, reads, writes, fn):
        self._emit_waits(q, reads, writes)
        k = id(dsem)
        self.dcount[k] = self.dcount.get(k, 0) + 16
        tok = (dsem, self.dcount[k])
        self.prog[q].append(lambda e, fn=fn, dsem=dsem: fn(e).then_inc(dsem, 16))
        self._mark(reads, writes, tok)
        return tok

    def final_wait(self, q, tok):
        sem, val = tok
        self.prog[q].append(lambda e, sem=sem, val=val: e.wait_ge(sem, val))


class Rot:
    def __init__(self, items):
        self.items = items
        self.i = 0

    def next(self):
        it = self.items[self.i % len(self.items)]
        self.i += 1
        return it


def build(T):
    NT = T // TT
    nc = bass.Bass("TRN2", target_bir_lowering=False)

    def din(name, shape):
        return nc.dram_tensor(name, shape, F32, kind="ExternalInput").ap()

    xT = din("xT", [D, T])
    pT = din("pT", [PLE, T])
    w_in = din("w_in", [D, 8192])
    w_a = din("w_a", [D, D])
    w_b = din("w_b", [D, D])
    w_o = din("w_o", [D, D])
    w_up = din("w_up", [D, 2 * DFF])
    w_dn = din("w_dn", [DFF, D])
    w_pg = din("w_pg", [D, D])
    w_pp = din("w_pp", [PLE, D])
    wsT_d = din("wsT", [128, 8 * 128])
    cst_d = din("cst", [128, NCST])
    bsb_d = din("bsb", [128, 1024])
    outT = nc.dram_tensor("outT", [D, T], F32, kind="ExternalOutput").ap()

    def kview(w):
        return w.rearrange("(k p) c -> p k c", p=128)

    xTv = kview(xT)
    pTv = kview(pT)
    outTv = kview(outT)

    es = ExitStack()
    with es:
        fw = FW(nc, es)

        def sb(name, shape, dt):
            return es.enter_context(nc.sbuf_tensor("sb_" + name, shape, dt))

        def sem(name):
            return es.enter_context(nc.semaphore(name))

        arena = sb("arena", [128, 24, TT], BF16)
        G = [Buf(f"g{i}") for i in range(24)]
        xbfs = [sb(f"xbf{q}", [128, KC, TT], BF16) for q in range(2)]
        XBs = [[Buf(f"xb{q}_{i}") for i in range(KC)] for q in range(2)]
        resid = sb("resid", [128, KC, TT], F32)
        R = [Buf(f"r{i}") for i in range(KC)]
        vst = resid[:].rearrange("p (b h) t -> p b (h t)", h=2)
        vn = sb("vn", [128, 4, 1024], BF16)
        VN = [Buf(f"vn{i}") for i in range(4)]
        mT = vn[:].rearrange("p b (h t) -> p (b h) t", h=2)
        ktT = sb("ktT", [128, KC, TT], BF16)
        KT = [Buf(f"kt{i}") for i in range(KC)]
        khT = sb("khT", [128, 2, TT], BF16)
        KHTs = [(khT[:, i, :], Buf(f"kht{i}")) for i in range(2)]
        khA = sb("khA", [128, 2, 4, 128], BF16)
        khB = sb("khB", [128, 2, 4, 128], BF16)
        KHABs = [(khA[:, i], khB[:, i], Buf(f"khab{i}")) for i in range(2)]
        itm = sb("itm", [128, 4, 1024], BF16)
        ITM = [Buf(f"itm{i}") for i in range(4)]
        att = sb("att", [128, 2, TT], BF16)
        ATTs = [(att[:, i, :], Buf(f"att{i}")) for i in range(2)]
        s32 = sb("s32", [128, 8, 128], F32)
        S32 = [Buf(f"s32_{i}") for i in range(8)]
        sh = sb("sh", [128, 9, 128], F32)
        SH = Buf("sh")
        sbf = sb("sbf", [128, 2, 8, 128], BF16)
        SBFs = [(sbf[:, i], Buf(f"sbf{i}")) for i in range(2)]
        ft = sb("ft", [128, 2, 5, TT], F32)
        FT = [[(ft[:, s_, k_, :], Buf(f"ft{s_}_{k_}")) for k_ in range(5)] for s_ in range(2)]
        sga = sb("sga", [128, KC, TT], BF16)
        SGA = [Buf(f"sga{i}") for i in range(KC)]
        sgb = sb("sgb", [128, KC, TT], BF16)
        SGB = [Buf(f"sgb{i}") for i in range(KC)]
        NTMP = 6
        t32 = sb("t32", [128, NTMP, TT], F32)
        T32 = Rot([(t32[:, i, :], Buf(f"t32_{i}")) for i in range(NTMP)])
        tbf = sb("tbf", [128, 4, TT], BF16)
        TBF = Rot([(tbf[:, i, :], Buf(f"tbf{i}")) for i in range(4)])
        gs = sb("gs", [128, 2, TT + 2], F32)
        GS = Rot([(gs[:, i, :], Buf(f"gs{i}")) for i in range(2)])
        cy = sb("cy", [128, FC, 2], F32)
        CY = [Buf(f"cy{i}") for i in range(FC)]
        pbf = sb("pbf", [128, 2, TT], BF16)
        PB = Buf("pb")
        wpp = sb("wpp", [128, 2, D], BF16)
        WPP = Buf("wpp")
        cst = sb("cst", [128, NCST], F32)
        CST = Buf("cst")
        wsT = sb("wsT", [128, 8, 128], BF16)
        WST = Buf("wsT")
        bsp = sb("bsp", [128, 1024], BF16)
        BSP = Buf("bsp")
        ones = sb("ones", [128, 128], BF16)
        onesD = sb("onesD", [128, 128], BF16)
        ident = sb("ident", [128, 128], BF16)
        CONSTB = Buf("constb")
        mask4 = sb("mask4", [128, 4, 128], BF16)
        msk = sb("msk", [128, TT], F32)
        el = sb("el", [128, 8, 8], F32)
        EL = [Buf(f"el{i}") for i in range(8)]
        lbt = sb("lbt", [128, 8], F32)
        omlt = sb("omlt", [128, 8], F32)
        ldt = sb("ldt", [128, 8], F32)
        LB = Buf("lb")
        bnst = sb("bnst", [128, 4, 2, 6], F32)
        mv = sb("mv", [128, 4, 2], F32)
        rsv = sb("rsv", [128, 4], F32)
        nmv = sb("nmv", [128, 4], F32)
        LNS = [Buf(f"lns{i}") for i in range(4)]
        wslots = [sb(f"wsl{i}", [128, SLOT_CAP], BF16) for i in range(NSLOT)]
        WS = [Buf(f"ws{i}") for i in range(NSLOT)]

        pmm = [es.enter_context(nc.psum_tensor(f"pmm{i}", [128, TT], F32)) for i in range(4)]
        PMMB = [Buf(f"pmm{i}", True) for i in range(4)]
        MM = Rot([(pmm[i], PMMB[i]) for i in range(4)])
        FMM = Rot([(pmm[i], PMMB[i]) for i in range(2)])
        DBK2 = [(pmm[2], PMMB[2]), (pmm[3], PMMB[3])]
        pax = [es.enter_context(nc.psum_tensor(f"pax{i}", [128, TT], F32)) for i in range(3)]
        PAXB = [Buf(f"pax{i}", True) for i in range(3)]
        AUX = Rot([(pax[i], PAXB[i]) for i in range(3)])
        XBK = [(pax[0], PAXB[0]), (pax[1], PAXB[1])]
        OBK = (pax[2], PAXB[2])
        ptr = es.enter_context(nc.psum_tensor("ptr", [128, 1024], BF16))
        PTRB = Buf("ptr")
        PTRs = [(ptr[:, i * 512:(i + 1) * 512], PTRB) for i in range(2)]

        s_cst = sem("d_cst")
        s_cstp = sem("d_cstp")
        s_wpp = sem("d_wpp")
        s_x = [sem("d_x0"), sem("d_x1")]
        s_p = sem("d_p")
        s_r = sem("d_r")
        s_o = sem("d_o")
        s_o8 = [sem(f"d_o{i}") for i in range(KC)]
        s_wl = [sem(f"d_wl{i}") for i in range(NSLOT)]
        s_ws = [sem(f"d_ws{i}") for i in range(NSLOT)]
        s_wr = [sem(f"d_wr{i}") for i in range(NSLOT)]

        scratch = {}
        slot_rr = [0]

        def wload(key, src_view, kc, cols, tile_idx):
            si = slot_rr[0] % NSLOT
            slot_rr[0] += 1
            sl = wslots[si][:, 0:kc * cols].rearrange("p (k c) -> p k c", c=cols)
            b = WS[si]
            if tile_idx == 0:
                fw.dma("pool", s_wl[si], [], [b], lambda e, sl=sl, src_view=src_view: e.dma_start(out=sl, in_=src_view))
                if NT > 1:
                    dr = nc.dram_tensor("scr_" + key, [128, kc * cols], BF16)
                    db = Buf("scr_" + key)
                    scratch[key] = (dr, db)
                    drv = dr.ap().rearrange("p (k c) -> p k c", c=cols)
                    fw.dma("sp", s_ws[si], [b], [db], lambda e, sl=sl, drv=drv: e.dma_start(out=drv, in_=sl))
            else:
                dr, db = scratch[key]
                drv = dr.ap().rearrange("p (k c) -> p k c", c=cols)
                fw.dma("sp", s_wr[si], [db], [b], lambda e, sl=sl, drv=drv: e.dma_start(out=sl, in_=drv))
            return sl, b

        fw.dma("sp", s_cst, [], [CST], lambda e: e.dma_start(out=cst[:], in_=cst_d))
        fw.dma("pool", s_cstp, [], [WST], lambda e: e.dma_start(out=wsT[:].rearrange("p g t -> p (g t)"), in_=wsT_d))
        fw.dma("pool", s_wpp, [], [WPP], lambda e: e.dma_start(out=wpp[:], in_=kview(w_pp)))

        def cs(name, i=0, n=1):
            o = _CST[name] + i
            return cst[:, o:o + n]

        fw.op("dve", [], [CONSTB], lambda e: e.memset(ones[:], 1.0))
        fw.op("dve", [], [CONSTB], lambda e: e.memset(onesD[:], 1.0 / 1024.0))
        fw.op("dve", [], [CONSTB], lambda e: e.memset(ident[:], 1.0))
        fw.op("pool", [CONSTB], [CONSTB], lambda e: e.affine_select(
            out=ident[:], in_=ident[:], pattern=[[1, 128]], compare_op=ALU.is_equal, fill=0.0,
            base=0, channel_multiplier=-1))
        fw.op("dve", [], [CONSTB], lambda e: e.memset(mask4[:], 1.0))
        fw.op("pool", [CONSTB], [CONSTB], lambda e: e.affine_select(
            out=mask4[:], in_=mask4[:], pattern=[[0, 4], [1, 128]], compare_op=ALU.is_ge, fill=0.0,
            base=0, channel_multiplier=-1))
        fw.op("dve", [CONSTB], [CONSTB], lambda e: e.memset(mask4[0:64, :, 64:128], 0.0))
        fw.op("dve", [], [CONSTB], lambda e: e.memset(msk[:], 1.0))
        fw.op("dve", [CONSTB], [CONSTB], lambda e: e.memset(
            msk[:].rearrange("p (c s) -> p c s", s=64)[:, :, 0:1], 0.0))
        fw.op("dve", [WST], [WST], lambda e: e.memset(wsT[64:128, :, 0:64], 0.0))
        tb1, TB1 = T32.next()
        tb2, TB2 = T32.next()
        bhi = tbf[:].rearrange("p a t -> p (a t)")[:, 0:1024]
        BHI = [b for (_, b) in TBF.items[0:2]]
        blo = t32[:].rearrange("p a t -> p (a t)")[:, 0:1024]
        fw.op("dve", [], [BSP], lambda e: e.memset(bsp[:], 0.0))
        bsf = t32[:].rearrange("p a t -> p (a t)")[:, 2048:3072]
        BSF = [T32.items[4][1], T32.items[5][1]]
        s_bs = sem("d_bs")
        fw.dma("sp", s_bs, [], BSF, lambda e: e.dma_start(out=bsf, in_=bsb_d))
        fw.op("dve", BSF, BHI, lambda e: e.tensor_copy(out=bhi, in_=bsf))
        fw.op("dve", BSF + BHI, [TB1, TB2], lambda e: e.tensor_tensor(out=blo, in0=bsf, in1=bhi, op=ALU.subtract))
        fw.op("dve", BHI + [BSP], [BSP], lambda e: e.tensor_copy(out=bsp[0:1, :], in_=bhi[0:1, :]))
        fw.op("dve", [TB1, TB2, BSP], [BSP], lambda e: e.tensor_copy(out=bsp[32:33, :], in_=blo[32:33, :]))
        fw.op("dve", [CST], [LB], lambda e: e.tensor_tensor(out=ldt[:], in0=cs("l0", 0, 8), in1=cs("l1", 0, 8), op=ALU.subtract))
        fw.op("act", [LB], [LB], lambda e: e.activation(out=lbt[:], in_=ldt[:], func=AF.Sigmoid))
        fw.op("act", [LB], [LB], lambda e: e.activation(out=omlt[:], in_=ldt[:], func=AF.Sigmoid, scale=-1.0))
        fw.op("pool", [], S32, lambda e: e.memset(s32[:], 0.0))
        fw.op("pool", [], CY, lambda e: e.memset(cy[:], 0.0))
        fw.op("pool", [], [b for (_, _, b) in KHABs], lambda e: e.memset(khA[:], 0.0))
        fw.op("pool", [], [b for (_, _, b) in KHABs], lambda e: e.memset(khB[:], 0.0))

        win_v = kview(w_in)
        wa_v = kview(w_a)
        wb_v = kview(w_b)
        wo_v = kview(w_o)
        wup_v = kview(w_up)
        wdn_v = kview(w_dn)
        wpg_v = kview(w_pg)

        def mm8(e, out, slot, j, rhs3, nk=KC):
            last = None
            for k in range(nk):
                last = e.matmul(out, lhsT=slot[:, k, j * 128:(j + 1) * 128], rhs=rhs3[:, k, :],
                                start=(k == 0), stop=(k == nk - 1))
            return last

        def ln_finalize(mb, MB, eb, EB):
            msq, MSQ = T32.next()
            fw.op("act", [MB], [MSQ], lambda e: e.activation(out=msq, in_=mb[:], func=AF.Square))
            var, VAR = T32.next()
            fw.op("dve", [EB, MSQ], [VAR], lambda e: e.tensor_tensor(out=var, in0=eb[:], in1=msq, op=ALU.subtract))
            fw.op("act", [CONSTB, VAR], [VAR], lambda e: e.activation(out=var, in_=var, func=AF.Ln, bias=epsln[:], scale=1.0))
            fw.op("act", [VAR], [VAR], lambda e: e.activation(out=var, in_=var, func=AF.Exp, scale=-0.5))
            nm, NM = T32.next()
            fw.op("dve", [MB, VAR], [NM], lambda e: e.scalar_tensor_tensor(
                out=nm, in0=mb[:], scalar=-1.0, in1=var, op0=ALU.mult, op1=ALU.mult))
            return var, VAR, nm, NM

        epsln = sb("epsln", [128, 1], F32)
        epsrms = sb("epsrms", [128, 1], F32)
        fw.op("dve", [], [CONSTB], lambda e: e.memset(epsln[:], LN_EPS))
        fw.op("dve", [], [CONSTB], lambda e: e.memset(epsrms[:], RMS_EPS))

        out_toks = []

        def prefetch_x(tn):
            if tn >= NT:
                return
            q = tn % 2
            tsl_ = slice(tn * TT, tn * TT + TT)
            fw.dma("pool", s_x[q], [], XBs[q], lambda e: e.dma_start(out=xbfs[q][:], in_=xTv[:, :, tsl_]))

        def tile_gen(tt):
            PLE = "pool" if tt > 0 else "dve"
            xbf = xbfs[tt % 2]
            XB = XBs[tt % 2]
            t0 = tt * TT
            tsl = slice(t0, t0 + TT)

            def fm_units(gi, evac, banks):
                units = []
                hold = {}
                for cbk in range(4):
                    for j in range(2):
                        def unit(cbk=cbk, j=j):
                            if j == 0:
                                c0 = gi * 1024 + cbk * 256
                                hold[cbk] = wload(f"win{gi}_{cbk}", win_v[:, :, c0:c0 + 256], KC, 256, tt)
                            sl, SL = hold[cbk]
                            bank, BANK = banks.next()
                            fw.op("pe", [SL] + XB, [BANK], lambda e, bank=bank, sl=sl, j=j: mm8(e, bank[:], sl, j, xbf))
                            evac(cbk * 2 + j, bank, BANK)
                        units.append(unit)
                return units

            def tm_units(gi, evac, banks):
                units = []
                hold = {}
                for cbk in range(4):
                    for tp in range(2):
                        def unit(cbk=cbk, tp=tp):
                            if tp == 0:
                                c0 = gi * 1024 + cbk * 256
                                hold[cbk] = wload(f"win{gi}_{cbk}", win_v[:, :, c0:c0 + 256], KC, 256, tt)
                            sl, SL = hold[cbk]
                            bank, BANK = banks.next()

                            def f(e, bank=bank, sl=sl, tp=tp):
                                last = None
                                for ti in range(2):
                                    tb = tp * 2 + ti
                                    for k in range(KC):
                                        last = e.matmul(bank[:, ti * 256:(ti + 1) * 256],
                                                        lhsT=xbf[:, k, tb * 128:(tb + 1) * 128], rhs=sl[:, k, :],
                                                        start=(k == 0), stop=(k == KC - 1))
                                return last
                            fw.op("pe", [SL] + XB, [BANK], f)
                            evac(cbk, tp, bank, BANK)
                        units.append(unit)
                return units

            def ev_u(cc, bank, BANK):
                fw.op("act", [BANK], [G[cc]], lambda e: e.activation(out=arena[:, cc, :], in_=bank[:], func=AF.Gelu_apprx_tanh))

            def ev_v(cbk, tp, bank, BANK):
                fw.op("act", [BANK], [R[4 * tp + 0], R[4 * tp + 1], R[4 * tp + 2], R[4 * tp + 3]],
                      lambda e: e.activation(out=vst[:, 2 * tp:2 * tp + 2, cbk * 256:(cbk + 1) * 256],
                                             in_=bank[:].rearrange("p (a c) -> p a c", c=256),
                                             func=AF.Gelu_apprx_tanh))

            def ev_q(cc, bank, BANK):
                fw.op("act", [BANK], [G[8 + cc]], lambda e: e.activation(out=arena[:, 8 + cc, :], in_=bank[:], func=AF.Silu))

            def ev_og(cc, bank, BANK):
                fw.op("act", [BANK], [G[16 + cc]], lambda e: e.activation(out=arena[:, 16 + cc, :], in_=bank[:], func=AF.Silu))

            def ev_i(cbk, tp, bank, BANK):
                fw.op("act", [BANK], [ITM[2 * tp], ITM[2 * tp + 1]],
                      lambda e: e.activation(out=itm[:, 2 * tp:2 * tp + 2, cbk * 256:(cbk + 1) * 256],
                                             in_=bank[:].rearrange("p (a c) -> p a c", c=256), func=AF.Copy))

            def ev_ga(cc, bank, BANK):
                fw.op("act", [BANK], [SGA[cc]], lambda e: e.activation(out=sga[:, cc, :], in_=bank[:], func=AF.Sigmoid))

            def ev_gb(cc, bank, BANK):
                fw.op("act", [BANK], [SGB[cc]], lambda e: e.activation(out=sgb[:, cc, :], in_=bank[:], func=AF.Sigmoid))

            yield tm_units(4, ev_i, MM)
            yield fm_units(2, ev_q, MM) + fm_units(5, ev_og, MM)
            prefetch_x(tt + 1)
            fw.dma("pool", s_p, [], [PB], lambda e: e.dma_start(out=pbf[:], in_=pTv[:, :, tsl]))

            def sgu_S0():
                for tb in range(4):
                    Rb = [R[2 * tb], R[2 * tb + 1]]
                    for hh in range(2):
                        fw.op("dve", Rb, [LNS[tb]], lambda e, hh=hh, tb=tb: e.bn_stats(out=bnst[:, tb, hh, :], in_=vst[:, tb, hh * 512:(hh + 1) * 512]))
                    fw.op("dve", [LNS[tb]], [LNS[tb]], lambda e, tb=tb: e.bn_aggr(out=mv[:, tb, :], in_=bnst[:, tb].rearrange("p a s -> p (a s)")))

            def sgu_S1():
                fw.op("act", LNS + [CONSTB], LNS, lambda e: e.activation(out=rsv[:], in_=mv[:, :, 1], func=AF.Ln, bias=epsln[:], scale=1.0))
                fw.op("act", LNS, LNS, lambda e: e.activation(out=rsv[:], in_=rsv[:], func=AF.Exp, scale=-0.5))
                fw.op("dve", LNS, LNS, lambda e: e.scalar_tensor_tensor(out=nmv[:], in0=mv[:, :, 0], scalar=-1.0, in1=rsv[:], op0=ALU.mult, op1=ALU.mult))

            def sgu_S2():
                for tb in range(4):
                    Rb = [R[2 * tb], R[2 * tb + 1]]
                    fw.op("act", Rb + [LNS[tb]], Rb, lambda e, tb=tb: e.activation(
                        out=vst[:, tb, :], in_=vst[:, tb, :], func=AF.Identity, bias=nmv[:, tb:tb + 1], scale=rsv[:, tb:tb + 1]))

            def sgu_S3(tbs):
                for tb in tbs:
                    Rb = [R[2 * tb], R[2 * tb + 1]]
                    fw.op("dve", Rb + [CST], Rb, lambda e, tb=tb: e.tensor_tensor(out=vst[:, tb, :], in0=vst[:, tb, :], in1=cs("gv", 0, 1024), op=ALU.mult))
                    fw.op("dve", Rb + [CST], [VN[tb]], lambda e, tb=tb: e.tensor_tensor(out=vn[:, tb, :], in0=vst[:, tb, :], in1=cs("bv", 0, 1024), op=ALU.add))

            def sgu_unit(tb):
                for gq in range(2):
                    bank, BANK = FMM.next()

                    def f(e, bank=bank, gq=gq):
                        last = None
                        for gi in range(4):
                            g = gq * 4 + gi
                            e.matmul(bank[:, gi * 128:(gi + 1) * 128], lhsT=vn[:, tb, g * 128:(g + 1) * 128],
                                     rhs=wsT[:, g, :], start=True, stop=False)
                            last = e.matmul(bank[:, gi * 128:(gi + 1) * 128], lhsT=ones[:, :],
                                            rhs=bsp[:, g * 128:(g + 1) * 128], start=False, stop=True)
                        return last
                    fw.op("pe", [VN[tb], WST, BSP, CONSTB], [BANK], f)
                    Gq = G[gq * 4:gq * 4 + 4]
                    fw.op("dve", [BANK] + Gq, Gq, lambda e, bank=bank, gq=gq: e.tensor_tensor(
                        out=arena[:, gq * 4:gq * 4 + 4, tb * 128:(tb + 1) * 128],
                        in0=bank[:].rearrange("p (a t) -> p a t", t=128),
                        in1=arena[:, gq * 4:gq * 4 + 4, tb * 128:(tb + 1) * 128], op=ALU.mult))

            nop = lambda: None

            def hgrn_E0(p):
                if p >= 4:
                    return
                sl, SL = wload(f"win3_{p}", win_v[:, :, 3072 + p * 256:3072 + p * 256 + 256], KC, 256, tt)
                for s in range(2):
                    xb_, XB_ = XBK[s]
                    fw.op("pe", [SL] + XB, [XB_], lambda e, xb_=xb_, s=s, sl=sl: mm8(e, xb_[:], sl, s, xbf))

            def hgrn_head(p, fill, tails, hooks):
                hs = (2 * p, 2 * p + 1)
                for s, h in enumerate(hs):
                    xb_, XB_ = XBK[s]
                    tA, TA = FT[s][0]
                    tB, TB = FT[s][1]
                    fw.op("act", [XB_], [TA], lambda e, tA=tA, xb_=xb_: e.activation(out=tA, in_=xb_[:], func=AF.Sigmoid))
                    fw.op("act", [XB_], [TB], lambda e, tB=tB, xb_=xb_: e.activation(out=tB, in_=xb_[:], func=AF.Sigmoid, scale=-1.0))
                hooks.get("E1", nop)()
                for s, h in enumerate(hs):
                    tA, TA = FT[s][0]
                    fw.op("act", [TA, LB], [TA], lambda e, tA=tA, h=h: e.activation(
                        out=tA, in_=tA, func=AF.Identity, bias=lbt[:, h:h + 1], scale=omlt[:, h:h + 1]))
                for s, h in enumerate(hs):
                    tA, TA = FT[s][0]
                    fw.op("act", [TA], [TA], lambda e, tA=tA: e.activation(out=tA, in_=tA, func=AF.Ln))
                for s, h in enumerate(hs):
                    tA, TA = FT[s][0]
                    tC, TC = FT[s][2]
                    fw.op("dve", [TA, CONSTB], [TC], lambda e, tA=tA, tC=tC: e.tensor_tensor_scan(
                        out=tC, data0=msk[:], data1=tA, initial=0.0, op0=ALU.mult, op1=ALU.add))
                hooks.get("E4", nop)()
                for s, h in enumerate(hs):
                    tA, TA = FT[s][0]
                    tC, TC = FT[s][2]
                    fw.op("act", [TC], [TA], lambda e, tA=tA, tC=tC: e.activation(out=tA, in_=tC, func=AF.Exp))
                    fw.op("act", [TC], [TC], lambda e, tC=tC: e.activation(out=tC, in_=tC, func=AF.Exp, scale=-1.0))
                for s, h in enumerate(hs):
                    tA, TA = FT[s][0]
                    tB, TB = FT[s][1]
                    tC, TC = FT[s][2]
                    kh, KH = KHTs[s]
                    fw.op("dve", [G[8 + h], TA], [G[8 + h]], lambda e, tA=tA, h=h: e.tensor_tensor(
                        out=arena[:, 8 + h, :], in0=arena[:, 8 + h, :], in1=tA, op=ALU.mult))
                    fw.op("dve", [TB, LB, TC], [TB], lambda e, tB=tB, tC=tC, h=h: e.scalar_tensor_tensor(
                        out=tB, in0=tB, scalar=omlt[:, h:h + 1], in1=tC, op0=ALU.mult, op1=ALU.mult))
                    fw.op("dve", [TA], [EL[h]], lambda e, tA=tA, h=h: e.tensor_copy(
                        out=el[:, h, :], in_=tA.rearrange("p (c s) -> p c s", s=64)[:, :, 63]))
                    fw.op("dve", [TB, EL[h]], [KH], lambda e, tB=tB, kh=kh, h=h: e.tensor_tensor(
                        out=kh.rearrange("p (c s) -> p c s", s=64), in0=tB.rearrange("p (c s) -> p c s", s=64),
                        in1=el[:, h, :].unsqueeze(2).to_broadcast([128, 8, 64]), op=ALU.mult))
                tails[0]()
                tails[1]()
                hooks.get("mid", nop)()
                for s, h in enumerate(hs):
                    tB, TB = FT[s][1]
                    kh, KH = KHTs[s]
                    pt, PT_ = PTRs[s]
                    fw.op("act", [TB], [KT[h]], lambda e, tB=tB, h=h: e.activation(out=ktT[:, h, :], in_=tB, func=AF.Copy))
                    fw.op("pe", [KH, CONSTB], [PT_], lambda e, kh=kh, pt=pt: [e.transpose(
                        out=pt[:, b * 128:(b + 1) * 128], in_=kh[:, b * 128:(b + 1) * 128], identity=ident[:]) for b in range(4)][-1])
                for s, h in enumerate(hs):
                    pt, PT_ = PTRs[s]
                    ka, kb, KAB = KHABs[s]
                    fw.op("act", [PT_], [KAB], lambda e, ka=ka, pt=pt: e.activation(
                        out=ka[0:64], in_=pt[0:64, :].rearrange("p (b c) -> p b c", c=128), func=AF.Copy))
                    fw.op("dve", [PT_, KAB], [KAB], lambda e, kb=kb, pt=pt: e.tensor_copy(
                        out=kb[64:128], in_=pt[64:128, :].rearrange("p (b c) -> p b c", c=128)))
                hooks.get("E7", nop)()
                fill[0]()
                fill[1]()
                hooks.get("E8", nop)()
                for s, h in enumerate(hs):
                    xb_, XB_ = XBK[s]
                    ka, kb, KAB = KHABs[s]
                    at, AT_ = ATTs[s]
                    sbh, SBH = SBFs[s]
                    fw.op("pe", [KT[h], G[8 + h]], [XB_], lambda e, xb_=xb_, h=h: [e.matmul(
                        xb_[:, b * 128:(b + 1) * 128], lhsT=ktT[:, h, b * 128:(b + 1) * 128],
                        rhs=arena[:, 8 + h, b * 128:(b + 1) * 128], start=True, stop=True) for b in range(4)][-1])
                    for half in range(2):
                        dbk, DBK = DBK2[half]

                        def f(e, dbk=dbk, half=half, ka=ka, kb=kb, h=h):
                            last = None
                            for ci in range(4):
                                c = half * 4 + ci
                                blk = c // 2
                                src = ka if (c % 2 == 0) else kb
                                last = e.matmul(dbk[:, ci * 128:(ci + 1) * 128], lhsT=src[:, blk, :],
                                                rhs=itm[:, blk, h * 128:(h + 1) * 128], start=True, stop=True)
                            return last
                        fw.op("pe", [KAB] + ITM, [DBK], f)
                    fw.op("dve", [XB_, CONSTB], [AT_], lambda e, at=at, xb_=xb_: e.tensor_tensor(
                        out=at, in0=xb_[:], in1=mask4[:].rearrange("p b t -> p (b t)"), op=ALU.mult))
                    fw.op(PLE, [S32[h], SH], [SH], lambda e, h=h: e.tensor_copy(out=sh[:, 0, :], in_=s32[:, h, :]))
                    for c in range(8):
                        dbk, DBK = DBK2[c // 4]
                        ci = c % 4
                        fw.op("dve", [SH, EL[h], DBK], [SH], lambda e, c=c, dbk=dbk, ci=ci, h=h: e.scalar_tensor_tensor(
                            out=sh[:, c + 1, :], in0=sh[:, c, :], scalar=el[:, h, c:c + 1],
                            in1=dbk[:, ci * 128:(ci + 1) * 128], op0=ALU.mult, op1=ALU.add))
                    fw.op(PLE, [SH], [S32[h]], lambda e, h=h: e.tensor_copy(out=s32[:, h, :], in_=sh[:, 8, :]))
                    fw.op("act", [SH], [SBH], lambda e, sbh=sbh: e.activation(out=sbh, in_=sh[:, 0:8, :], func=AF.Copy))
                    if s == 0:
                        fill[2]()
                        fill[3]()
                        fill[4]()
                fill[5]()
                hgrn_E0(p + 1)
                fill[6]()
                fill[7]()
                tails[2]()

            def hgrn_tail(p):
                hs = (2 * p, 2 * p + 1)

                def T0():
                  for s, h in enumerate(hs):
                    at, AT_ = ATTs[s]
                    sbh, SBH = SBFs[s]
                    ob, OB = OBK
                    tO, TO = FT[s][3]

                    def fo(e, at=at, sbh=sbh, h=h):
                        last = None
                        for b in range(4):
                            e.matmul(ob[:, b * 128:(b + 1) * 128], lhsT=itm[:, b, h * 128:(h + 1) * 128],
                                     rhs=at[:, b * 128:(b + 1) * 128], start=True, stop=False)
                            for hf in range(2):
                                c = 2 * b + hf
                                cs_ = slice(b * 128 + hf * 64, b * 128 + hf * 64 + 64)
                                last = e.matmul(ob[:, cs_], lhsT=sbh[:, c, :], rhs=arena[:, 8 + h, cs_],
                                                start=False, stop=(hf == 1))
                        return last
                    fw.op("pe", ITM + [AT_, SBH, G[8 + h]], [OB], fo)
                    sq, SQ = TBF.next()
                    fw.op("act", [OB], [SQ], lambda e, sq=sq: e.activation(out=sq, in_=ob[:], func=AF.Square))
                    fw.op("act", [OB, CST], [TO], lambda e, tO=tO, h=h: e.activation(
                        out=tO, in_=ob[:], func=AF.Identity, scale=cs("hg", h, 1)))
                    xb_, XB_ = XBK[s]
                    fw.op("pe", [SQ, CONSTB], [XB_], lambda e, xb_=xb_, sq=sq: e.matmul(xb_[:], lhsT=ones[:], rhs=sq, start=True, stop=True))

                def T1():
                  for s, h in enumerate(hs):
                    xb_, XB_ = XBK[s]
                    rs, RS = FT[s][4]
                    fw.op("act", [XB_, CONSTB], [RS], lambda e, rs=rs, xb_=xb_: e.activation(out=rs, in_=xb_[:], func=AF.Ln, bias=epsrms[:], scale=1.0 / 128.0))
                    fw.op("act", [RS], [RS], lambda e, rs=rs: e.activation(out=rs, in_=rs, func=AF.Exp, scale=-0.5))

                def T2():
                  for s, h in enumerate(hs):
                    tO, TO = FT[s][3]
                    rs, RS = FT[s][4]
                    fw.op("dve", [TO, RS], [RS], lambda e, rs=rs, tO=tO: e.tensor_tensor(out=rs, in0=tO, in1=rs, op=ALU.mult))
                    fw.op("dve", [RS, G[16 + h]], [KT[h]], lambda e, rs=rs, h=h: e.tensor_tensor(
                        out=ktT[:, h, :], in0=rs, in1=arena[:, 16 + h, :], op=ALU.mult))
                return (T0, T1, T2)

            fill_v = tm_units(1, ev_v, FMM)
            fill_u = fm_units(0, ev_u, FMM)
            fill_ga = fm_units(6, ev_ga, FMM)
            fill_gb = fm_units(7, ev_gb, FMM)

            fills = [fill_u, fill_v, fill_ga, fill_gb]
            hk = [{}, {},
                  {"E1": sgu_S0, "E4": sgu_S1, "mid": sgu_S2},
                  {"E1": lambda: sgu_S3((0, 1)), "E4": lambda: sgu_S3((2, 3)),
                   "E7": lambda: (sgu_unit(0), sgu_unit(1)), "E8": lambda: (sgu_unit(2), sgu_unit(3))}]
            tails = (nop, nop, nop)
            hgrn_E0(0)
            for p in range(4):
                hgrn_head(p, fills[p], tails, hk[p])
                tails = hgrn_tail(p)
            for t_ in tails:
                t_()

            for dp in range(4):
                c0 = dp * 256
                sa, SA = wload(f"wa{dp}", wa_v[:, :, c0:c0 + 256], KC, 256, tt)
                sbw, SBW = wload(f"wb{dp}", wb_v[:, :, c0:c0 + 256], KC, 256, tt)
                t1s = []
                for j in range(2):
                    dj = dp * 2 + j
                    ba, BA = MM.next()
                    fw.op("pe", [SA] + G[0:8], [BA], lambda e, ba=ba, sa=sa, j=j: mm8(e, ba[:], sa, j, arena[:, 0:8, :]))
                    t1, T1 = T32.next()
                    fw.op("dve", [BA, SGA[dj]], [T1], lambda e, t1=t1, ba=ba, dj=dj: e.tensor_tensor(out=t1, in0=ba[:], in1=sga[:, dj, :], op=ALU.mult))
                    t1s.append((t1, T1))
                for j in range(2):
                    dj = dp * 2 + j
                    t1, T1 = t1s[j]
                    bb, BB = MM.next()
                    fw.op("pe", [SBW] + KT, [BB], lambda e, bb=bb, sbw=sbw, j=j: mm8(e, bb[:], sbw, j, ktT))
                    t2, T2 = T32.next()
                    fw.op("dve", [BB, SGB[dj]], [T2], lambda e, t2=t2, bb=bb, dj=dj: e.tensor_tensor(out=t2, in0=bb[:], in1=sgb[:, dj, :], op=ALU.mult))
                    fw.op("dve", [T1, T2], [VN[dj // 2]], lambda e, t1=t1, t2=t2, dj=dj: e.tensor_tensor(out=mT[:, dj, :], in0=t1, in1=t2, op=ALU.add))

            fw.dma("pool", s_r, [], R, lambda e, tsl=tsl: e.dma_start(out=resid[:], in_=xTv[:, :, tsl]))

            def layer_norm_fm(gname, bname, write_bf):
                pass

            def stats_accum(dj, mb, MB, eb, EB):
                yb, YB = TBF.next()
                fw.op("act", [R[dj]], [YB], lambda e: e.activation(out=yb, in_=resid[:, dj, :], func=AF.Copy))
                ys, YS = TBF.next()
                fw.op("act", [R[dj]], [YS], lambda e: e.activation(out=ys, in_=resid[:, dj, :], func=AF.Square))

                def pe_part():
                    fw.op("pe", [YB, CONSTB], [MB], lambda e: e.matmul(mb[:], lhsT=onesD[:], rhs=yb, start=(dj == 0), stop=(dj == KC - 1)))
                    fw.op("pe", [YS, CONSTB], [EB], lambda e: e.matmul(eb[:], lhsT=onesD[:], rhs=ys, start=(dj == 0), stop=(dj == KC - 1)))
                return pe_part

            def normalize(gname, bname, rs, RS, nm, NM, bf_out):
                for dj in range(KC):
                    normalize_chunk(dj, gname, bname, rs, RS, nm, NM, bf_out)

            def normalize_chunk(dj, gname, bname, rs, RS, nm, NM, bf_out):
                if True:
                    fw.op("dve", [R[dj], RS], [R[dj]], lambda e, dj=dj: e.tensor_tensor(out=resid[:, dj, :], in0=resid[:, dj, :], in1=rs, op=ALU.mult))
                    fw.op(PLE, [R[dj], NM], [R[dj]], lambda e, dj=dj: e.tensor_tensor(out=resid[:, dj, :], in0=resid[:, dj, :], in1=nm, op=ALU.add))
                    if bf_out:
                        fw.op("act", [R[dj], CST], [XB[dj]], lambda e, dj=dj: e.activation(
                            out=xbf[:, dj, :], in_=resid[:, dj, :], func=AF.Identity,
                            bias=cs(bname, dj, 1), scale=cs(gname, dj, 1)))
                    fw.op("act", [R[dj], CST], [R[dj]], lambda e, dj=dj: e.activation(
                        out=resid[:, dj, :], in_=resid[:, dj, :], func=AF.Identity,
                        bias=cs(bname, dj, 1), scale=cs(gname, dj, 1)))


            mb, MB = AUX.next()
            eb, EB = AUX.next()
            pend = None
            for dp in range(4):
                c0 = dp * 256
                so, SO = wload(f"wo{dp}", wo_v[:, :, c0:c0 + 256], KC, 256, tt)
                for j in range(2):
                    dj = dp * 2 + j
                    bo, BO = MM.next()
                    fw.op("pe", [SO] + VN, [BO], lambda e, bo=bo, so=so, j=j: mm8(e, bo[:], so, j, mT))
                    if pend is not None:
                        pend()
                    fw.op("dve", [R[dj], BO], [R[dj]], lambda e, dj=dj, bo=bo: e.scalar_tensor_tensor(
                        out=resid[:, dj, :], in0=resid[:, dj, :], scalar=ALPHA, in1=bo[:], op0=ALU.mult, op1=ALU.add))
                    pend = stats_accum(dj, mb, MB, eb, EB)
            pend()
            yield
            st1 = {}

            def ln1_fin(mb=mb, MB=MB, eb=eb, EB=EB):
                st1["v"] = ln_finalize(mb, MB, eb, EB)

            def ln1_chunk(dj):
                rs, RS, nm, NM = st1["v"]
                normalize_chunk(dj, "ln1g", "ln1b", rs, RS, nm, NM, True)
            yield (ln1_fin, ln1_chunk)

            for fp in range(FC // 2):
                c0 = fp * 256
                sg_, SG_ = wload(f"wug{fp}", wup_v[:, :, c0:c0 + 256], KC, 256, tt)
                sv_, SV_ = wload(f"wuv{fp}", wup_v[:, :, DFF + c0:DFF + c0 + 256], KC, 256, tt)
                for j in range(2):
                    fc = fp * 2 + j
                    bg, BG = MM.next()
                    fw.op("pe", [SG_] + XB, [BG], lambda e, bg=bg, sg_=sg_, j=j: mm8(e, bg[:], sg_, j, xbf))
                    bv, BV = MM.next()
                    fw.op("pe", [SV_] + XB, [BV], lambda e, bv=bv, sv_=sv_, j=j: mm8(e, bv[:], sv_, j, xbf))
                    g_, GS_ = GS.next()
                    fw.op(PLE, [CY[fc], GS_], [GS_], lambda e, g_=g_, fc=fc: e.tensor_copy(out=g_[:, 0:2], in_=cy[:, fc, :]))
                    fw.op("act", [BG, GS_], [GS_], lambda e, g_=g_, bg=bg: e.activation(out=g_[:, 2:TT + 2], in_=bg[:], func=AF.Copy))
                    fw.op(PLE, [GS_, CY[fc]], [CY[fc]], lambda e, g_=g_, fc=fc: e.tensor_copy(out=cy[:, fc, :], in_=g_[:, TT:TT + 2]))
                    ac, AC = T32.next()
                    cw0 = _CST["cw"] + fc * 3
                    fw.op("act", [BG, CST], [AC], lambda e, ac=ac, bg=bg, cw0=cw0, fc=fc: e.activation(
                        out=ac, in_=bg[:], func=AF.Identity, bias=cs("cb", fc, 1), scale=cst[:, cw0 + 2:cw0 + 3]))
                    fw.op("dve", [GS_, CST, AC], [AC], lambda e, ac=ac, g_=g_, cw0=cw0: e.scalar_tensor_tensor(
                        out=ac, in0=g_[:, 1:TT + 1], scalar=cst[:, cw0 + 1:cw0 + 2], in1=ac, op0=ALU.mult, op1=ALU.add))
                    fw.op("dve", [GS_, CST, AC], [AC], lambda e, ac=ac, g_=g_, cw0=cw0: e.scalar_tensor_tensor(
                        out=ac, in0=g_[:, 0:TT], scalar=cst[:, cw0:cw0 + 1], in1=ac, op0=ALU.mult, op1=ALU.add))
                    fw.op("act", [AC], [AC], lambda e, ac=ac: e.activation(out=ac, in_=ac, func=AF.Gelu_apprx_tanh))
                    fw.op("dve", [AC, BV], [G[fc]], lambda e, ac=ac, bv=bv, fc=fc: e.tensor_tensor(
                        out=arena[:, fc, :], in0=bv[:], in1=ac, op=ALU.mult))

            mb, MB = AUX.next()
            eb, EB = AUX.next()
            pend = None
            for dp in range(4):
                sp_, SP_ = wload(f"wpg{dp}", wpg_v[:, :, dp * 256:dp * 256 + 256], KC, 256, tt)
                for j in range(2):
                    dj = dp * 2 + j
                    sd, SD = wload(f"wdn{dj}", wdn_v[:, :, dj * 128:dj * 128 + 128], FC, 128, tt)
                    bf_, BF_ = MM.next()
                    fw.op("pe", [SD] + G[0:FC], [BF_], lambda e, bf_=bf_, sd=sd: mm8(e, bf_[:], sd, 0, arena[:, 0:FC, :], nk=FC))
                    bpg, BPG = MM.next()
                    fw.op("pe", [SP_] + XB, [BPG], lambda e, bpg=bpg, sp_=sp_, j=j: mm8(e, bpg[:], sp_, j, xbf))
                    bpp, BPP = MM.next()
                    fw.op("pe", [WPP, PB], [BPP], lambda e, bpp=bpp, dj=dj: mm8(e, bpp[:], wpp[:, :, dj * 128:(dj + 1) * 128], 0, pbf, nk=2))
                    if pend is not None:
                        pend()
                    sgp, SGP = TBF.next()
                    fw.op("act", [BPG], [SGP], lambda e, sgp=sgp, bpg=bpg: e.activation(out=sgp, in_=bpg[:], func=AF.Sigmoid))
                    t1, T1 = T32.next()
                    fw.op("dve", [BPP, SGP], [T1], lambda e, t1=t1, bpp=bpp, sgp=sgp: e.tensor_tensor(out=t1, in0=bpp[:], in1=sgp, op=ALU.mult))
                    fw.op("dve", [R[dj], BF_], [R[dj]], lambda e, dj=dj, bf_=bf_: e.scalar_tensor_tensor(
                        out=resid[:, dj, :], in0=resid[:, dj, :], scalar=ALPHA, in1=bf_[:], op0=ALU.mult, op1=ALU.add))
                    fw.op(PLE, [R[dj], T1], [R[dj]], lambda e, dj=dj, t1=t1: e.tensor_tensor(
                        out=resid[:, dj, :], in0=resid[:, dj, :], in1=t1, op=ALU.add))
                    pend = stats_accum(dj, mb, MB, eb, EB)
            pend()
            yield
            st2 = {}

            def ln2_fin():
                st2["v"] = ln_finalize(mb, MB, eb, EB)

            def ln2_chunk(dj):
                rs, RS, nm, NM = st2["v"]
                normalize_chunk(dj, "ln2g", "ln2b", rs, RS, nm, NM, False)
                out_toks.append(fw.dma("act", s_o8[dj], [R[dj]], [], lambda e: e.dma_start(out=outTv[:, dj, tsl], in_=resid[:, dj, :])))

            def ln2_out():
                pass
            yield (ln2_fin, ln2_chunk, ln2_out)

        gens = [tile_gen(t_) for t_ in range(NT)]
        prefetch_x(0)
        for u_ in next(gens[0]):
            u_()
        for u_ in next(gens[0]):
            u_()
        for t_ in range(NT):
            g_t = gens[t_]
            next(g_t)
            units_i = list(next(gens[t_ + 1])) if t_ + 1 < NT else []
            fin1, chunk1 = next(g_t)
            fin1()
            for dj in range(KC):
                chunk1(dj)
                if dj < len(units_i):
                    units_i[dj]()
            next(g_t)
            units = list(next(gens[t_ + 1])) if t_ + 1 < NT else []
            fin, chunk, outd = next(g_t)
            fin()
            per = (len(units) + KC - 1) // KC if units else 0
            for dj in range(KC):
                chunk(dj)
                for u_ in units[dj * per:(dj + 1) * per]:
                    u_()
            outd()

        for tok_ in out_toks[-KC:]:
            fw.final_wait("pool", tok_)

        block = es.enter_context(nc.Block())

        @block.sync
        def _(e):
            for th in fw.prog["sp"]:
                th(e)

        @block.tensor
        def _(e):
            for th in fw.prog["pe"]:
                th(e)

        @block.scalar
        def _(e):
            for th in fw.prog["act"]:
                th(e)

        @block.vector
        def _(e):
            for th in fw.prog["dve"]:
                th(e)

        @block.gpsimd
        def _(e):
            for th in fw.prog["pool"]:
                th(e)
    return nc


def _pk(v):
    return np.ascontiguousarray(np.asarray(v, np.float32).reshape(-1, 128).T)


def _prep_shared(inp):
    g = lambda k: np.asarray(inp[k], np.float32)
    cst = np.zeros((128, NCST), np.float32)

    def put(name, arr):
        cst[:, _CST[name]:_CST[name] + arr.shape[1]] = arr
    put("ln1g", _pk(g("ln1_g")[0]))
    put("ln1b", _pk(g("ln1_b")[0]))
    put("ln2g", _pk(g("ln2_g")[0]))
    put("ln2b", _pk(g("ln2_b")[0]))
    put("hg", _pk(g("hgrn_norm_g")[0]))
    put("l0", _pk(g("hgrn_lb_logits")[0]))
    put("l1", _pk(g("hgrn_lb_logits")[1]))
    cw = g("ffn_conv_w")[0]
    cwp = np.stack([_pk(cw[j]) for j in range(3)], axis=2)
    put("cw", cwp.reshape(128, 66))
    put("cb", _pk(g("ffn_conv_b")[0]))
    put("gv", np.broadcast_to(g("sgu_norm_g")[0][None, :], (128, 1024)))
    put("bv", np.broadcast_to(g("sgu_norm_b")[0][None, :], (128, 1024)))
    wsT = np.ascontiguousarray(g("sgu_w_s")[0].transpose(2, 0, 1)).reshape(128, 1024)
    return {
        "w_in": np.ascontiguousarray(g("w_in")[0]),
        "w_a": np.ascontiguousarray(g("w_branch")[0, 0]),
        "w_b": np.ascontiguousarray(g("w_branch")[0, 1]),
        "w_o": np.ascontiguousarray(g("w_out")[0]),
        "w_up": np.ascontiguousarray(g("ffn_w_up")[0]),
        "w_dn": np.ascontiguousarray(g("ffn_w_down")[0]),
        "w_pg": np.ascontiguousarray(g("ple_w_gate")[0]),
        "w_pp": np.ascontiguousarray(g("ple_w_proj")[0]),
        "wsT": wsT,
        "cst": cst,
        "bsb": np.ascontiguousarray(np.broadcast_to(g("sgu_b_s")[0].reshape(1, 1024), (128, 1024))),
    }


_NC_CACHE = {}


def kernel(**inputs):
    x = np.asarray(inputs["x"], np.float32)
    p = np.asarray(inputs["p"], np.float32)
    B, T, _ = x.shape
    assert B == NCORES and T % TT == 0
    shared = _prep_shared(inputs)
    in_maps = []
    for b in range(B):
        m = dict(shared)
        m["xT"] = np.ascontiguousarray(x[b].T)
        m["pT"] = np.ascontiguousarray(p[0, b].T)
        in_maps.append(m)
    if T not in _NC_CACHE:
        _NC_CACHE[T] = build(T)
    nc = _NC_CACHE[T]
    res = run_bass_kernel_spmd(nc, in_maps, core_ids=list(range(NCORES)))
    out = np.stack([np.ascontiguousarray(r["outT"].T) for r in res.results], axis=0)
    return out.astype(np.float32)
```

```python
import numpy as np
from contextlib import ExitStack
import concourse.bass as bass
import concourse.mybir as mybir
from concourse.bass_utils import run_bass_kernel_spmd

F32 = mybir.dt.float32
BF16 = mybir.dt.bfloat16
AF = mybir.ActivationFunctionType
ALU = mybir.AluOpType

D = 1024
KC = 8
TT = 512
DFF = 2816
FC = 22
PLE = 256
NCORES = 8
ALPHA = float(2.0 ** 0.25)
LN_EPS = 1e-5
RMS_EPS = 1e-6
SLOT_CAP = 2816
NSLOT = 5

_CST = {}
_o = 0
for _n, _w in (("ln1g", 8), ("ln1b", 8), ("ln2g", 8), ("ln2b", 8), ("hg", 8), ("l0", 8), ("l1", 8),
               ("cw", 66), ("cb", 22), ("gv", 1024), ("bv", 1024)):
    _CST[_n] = _o
    _o += _w
NCST = _o


class Buf:
    __slots__ = ("name", "w", "r", "excl")

    def __init__(self, name, excl=False):
        self.name = name
        self.w = None
        self.r = {}
        self.excl = excl


class FW:
    ENG = ("pe", "act", "dve", "pool", "sp")

    def __init__(self, nc, es):
        self.nc = nc
        self.prog = {e: [] for e in self.ENG}
        self.esem = {e: es.enter_context(nc.semaphore("es_" + e)) for e in ("pe", "act", "dve", "pool")}
        self.cnt = {e: 0 for e in self.ENG}
        self.seen = {e: {} for e in self.ENG}
        self.dcount = {}
        self.nwait = 0

    def _waits(self, eng, reads, writes):
        need = {}

        def add(tok):
            if tok is None:
                return
            sem, val = tok
            k = id(sem)
            if k not in need or need[k][1] < val:
                need[k] = (sem, val)

        for b in reads:
            add(b.w)
            if b.excl:
                for tok in b.r.values():
                    add(tok)
        for b in writes:
            add(b.w)
            for tok in b.r.values():
                add(tok)
        out = []
        for k, (sem, val) in need.items():
            if eng == "pe" and sem is self.esem["pe"]:
                continue
            if self.seen[eng].get(k, 0) >= val:
                continue
            self.seen[eng][k] = val
            out.append((sem, val))
        return out

    def _mark(self, reads, writes, tok):
        k = id(tok[0])
        for b in reads:
            b.r[k] = tok
        for b in writes:
            b.w = tok
            b.r = {}

    def _emit_waits(self, eng, reads, writes):
        for sem, val in self._waits(eng, reads, writes):
            self.nwait += 1
            self.prog[eng].append(lambda e, sem=sem, val=val: e.wait_ge(sem, val))

    def op(self, eng, reads, writes, fn):
        self._emit_waits(eng, reads, writes)
        self.cnt[eng] += 1
        sem = self.esem[eng]
        tok = (sem, self.cnt[eng])
        self.prog[eng].append(lambda e, fn=fn, sem=sem: fn(e).then_inc(sem, 1))
        self._mark(reads, writes, tok)

    def dma(self, q, dsem, reads, writes, fn):
        self._emit_waits(q, reads, writes)
        k = id(dsem)
        self.dcount[k] = self.dcount.get(k, 0) + 16
        tok = (dsem, self.dcount[k])
        self.prog[q].append(lambda e, fn=fn, dsem=dsem: fn(e).then_inc(dsem, 16))
        self._mark(reads, writes, tok)
        return tok

    def final_wait(self, q, tok):
        sem, val = tok
        self.prog[q].append(lambda e, sem=sem, val=val: e.wait_ge(sem, val))


class Rot:
    def __init__(self, items):
        self.items = items
        self.i = 0

    def next(self):
        it = self.items[self.i % len(self.items)]
        self.i += 1
        return it


def build(T):
    NT = T // TT
    nc = bass.Bass("TRN2", target_bir_lowering=False)

    def din(name, shape):
        return nc.dram_tensor(name, shape, F32, kind="ExternalInput").ap()

    xT = din("xT", [D, T])
    pT = din("pT", [PLE, T])
    w_in = din("w_in", [D, 8192])
    w_a = din("w_a", [D, D])
    w_b = din("w_b", [D, D])
    w_o = din("w_o", [D, D])
    w_up = din("w_up", [D, 2 * DFF])
    w_dn = din("w_dn", [DFF, D])
    w_pg = din("w_pg", [D, D])
    w_pp = din("w_pp", [PLE, D])
    wsT_d = din("wsT", [128, 8 * 128])
    cst_d = din("cst", [128, NCST])
    bsb_d = din("bsb", [128, 1024])
    outT = nc.dram_tensor("outT", [D, T], F32, kind="ExternalOutput").ap()

    def kview(w):
        return w.rearrange("(k p) c -> p k c", p=128)

    xTv = kview(xT)
    pTv = kview(pT)
    outTv = kview(outT)

    es = ExitStack()
    with es:
        fw = FW(nc, es)

        def sb(name, shape, dt):
            return es.enter_context(nc.sbuf_tensor("sb_" + name, shape, dt))

        def sem(name):
            return es.enter_context(nc.semaphore(name))

        arena = sb("arena", [128, 24, TT], BF16)
        G = [Buf(f"g{i}") for i in range(24)]
        xbfs = [sb(f"xbf{q}", [128, KC, TT], BF16) for q in range(2)]
        XBs = [[Buf(f"xb{q}_{i}") for i in range(KC)] for q in range(2)]
        resid = sb("resid", [128, KC, TT], F32)
        R = [Buf(f"r{i}") for i in range(KC)]
        vst = resid[:].rearrange("p (b h) t -> p b (h t)", h=2)
        vn = sb("vn", [128, 4, 1024], BF16)
        VN = [Buf(f"vn{i}") for i in range(4)]
        mT = vn[:].rearrange("p b (h t) -> p (b h) t", h=2)
        ktT = sb("ktT", [128, KC, TT], BF16)
        KT = [Buf(f"kt{i}") for i in range(KC)]
        khT = sb("khT", [128, 2, TT], BF16)
        KHTs = [(khT[:, i, :], Buf(f"kht{i}")) for i in range(2)]
        khA = sb("khA", [128, 2, 4, 128], BF16)
        khB = sb("khB", [128, 2, 4, 128], BF16)
        KHABs = [(khA[:, i], khB[:, i], Buf(f"khab{i}")) for i in range(2)]
        itm = sb("itm", [128, 4, 1024], BF16)
        ITM = [Buf(f"itm{i}") for i in range(4)]
        att = sb("att", [128, 2, TT], BF16)
        ATTs = [(att[:, i, :], Buf(f"att{i}")) for i in range(2)]
        s32 = sb("s32", [128, 8, 128], F32)
        S32 = [Buf(f"s32_{i}") for i in range(8)]
        sh = sb("sh", [128, 9, 128], F32)
        SH = Buf("sh")
        sbf = sb("sbf", [128, 2, 8, 128], BF16)
        SBFs = [(sbf[:, i], Buf(f"sbf{i}")) for i in range(2)]
        ft = sb("ft", [128, 2, 5, TT], F32)
        FT = [[(ft[:, s_, k_, :], Buf(f"ft{s_}_{k_}")) for k_ in range(5)] for s_ in range(2)]
        sga = sb("sga", [128, KC, TT], BF16)
        SGA = [Buf(f"sga{i}") for i in range(KC)]
        sgb = sb("sgb", [128, KC, TT], BF16)
        SGB = [Buf(f"sgb{i}") for i in range(KC)]
        NTMP = 6
        t32 = sb("t32", [128, NTMP, TT], F32)
        T32 = Rot([(t32[:, i, :], Buf(f"t32_{i}")) for i in range(NTMP)])
        tbf = sb("tbf", [128, 4, TT], BF16)
        TBF = Rot([(tbf[:, i, :], Buf(f"tbf{i}")) for i in range(4)])
        gs = sb("gs", [128, 2, TT + 2], F32)
        GS = Rot([(gs[:, i, :], Buf(f"gs{i}")) for i in range(2)])
        cy = sb("cy", [128, FC, 2], F32)
        CY = [Buf(f"cy{i}") for i in range(FC)]
        pbf = sb("pbf", [128, 2, TT], BF16)
        PB = Buf("pb")
        wpp = sb("wpp", [128, 2, D], BF16)
        WPP = Buf("wpp")
        cst = sb("cst", [128, NCST], F32)
        CST = Buf("cst")
        wsT = sb("wsT", [128, 8, 128], BF16)
        WST = Buf("wsT")
        bsp = sb("bsp", [128, 1024], BF16)
        BSP = Buf("bsp")
        ones = sb("ones", [128, 128], BF16)
        onesD = sb("onesD", [128, 128], BF16)
        ident = sb("ident", [128, 128], BF16)
        CONSTB = Buf("constb")
        mask4 = sb("mask4", [128, 4, 128], BF16)
        msk = sb("msk", [128, TT], F32)
        el = sb("el", [128, 8, 8], F32)
        EL = [Buf(f"el{i}") for i in range(8)]
        lbt = sb("lbt", [128, 8], F32)
        omlt = sb("omlt", [128, 8], F32)
        ldt = sb("ldt", [128, 8], F32)
        LB = Buf("lb")
        bnst = sb("bnst", [128, 4, 2, 6], F32)
        mv = sb("mv", [128, 4, 2], F32)
        rsv = sb("rsv", [128, 4], F32)
        nmv = sb("nmv", [128, 4], F32)
        LNS = [Buf(f"lns{i}") for i in range(4)]
        wslots = [sb(f"wsl{i}", [128, SLOT_CAP], BF16) for i in range(NSLOT)]
        WS = [Buf(f"ws{i}") for i in range(NSLOT)]

        pmm = [es.enter_context(nc.psum_tensor(f"pmm{i}", [128, TT], F32)) for i in range(4)]
        PMMB = [Buf(f"pmm{i}", True) for i in range(4)]
        MM = Rot([(pmm[i], PMMB[i]) for i in range(4)])
        FMM = Rot([(pmm[i], PMMB[i]) for i in range(2)])
        DBK2 = [(pmm[2], PMMB[2]), (pmm[3], PMMB[3])]
        pax = [es.enter_context(nc.psum_tensor(f"pax{i}", [128, TT], F32)) for i in range(3)]
        PAXB = [Buf(f"pax{i}", True) for i in range(3)]
        AUX = Rot([(pax[i], PAXB[i]) for i in range(3)])
        FFB = Rot([(pmm[i], PMMB[i]) for i in range(4)] + [(pax[i], PAXB[i]) for i in range(3)])
        XBK = [(pax[0], PAXB[0]), (pax[1], PAXB[1])]
        OBK = (pax[2], PAXB[2])
        ptr = es.enter_context(nc.psum_tensor("ptr", [128, 1024], BF16))
        PTRB = Buf("ptr")
        PTRs = [(ptr[:, i * 512:(i + 1) * 512], PTRB) for i in range(2)]

        s_cst = sem("d_cst")
        s_cstp = sem("d_cstp")
        s_wpp = sem("d_wpp")
        s_x = [sem("d_x0"), sem("d_x1")]
        s_p = sem("d_p")
        s_r = sem("d_r")
        s_o = sem("d_o")
        s_wl = [sem(f"d_wl{i}") for i in range(NSLOT)]
        s_ws = [sem(f"d_ws{i}") for i in range(NSLOT)]
        s_wr = [sem(f"d_wr{i}") for i in range(NSLOT)]

        scratch = {}
        slot_rr = [0]

        def wload(key, src_view, kc, cols, tile_idx):
            si = slot_rr[0] % NSLOT
            slot_rr[0] += 1
            sl = wslots[si][:, 0:kc * cols].rearrange("p (k c) -> p k c", c=cols)
            b = WS[si]
            if tile_idx == 0:
                fw.dma("pool", s_wl[si], [], [b], lambda e, sl=sl, src_view=src_view: e.dma_start(out=sl, in_=src_view))
                if NT > 1:
                    dr = nc.dram_tensor("scr_" + key, [128, kc * cols], BF16)
                    db = Buf("scr_" + key)
                    scratch[key] = (dr, db)
                    drv = dr.ap().rearrange("p (k c) -> p k c", c=cols)
                    fw.dma("sp", s_ws[si], [b], [db], lambda e, sl=sl, drv=drv: e.dma_start(out=drv, in_=sl))
            else:
                dr, db = scratch[key]
                drv = dr.ap().rearrange("p (k c) -> p k c", c=cols)
                fw.dma("sp", s_wr[si], [db], [b], lambda e, sl=sl, drv=drv: e.dma_start(out=sl, in_=drv))
            return sl, b

        fw.dma("sp", s_cst, [], [CST], lambda e: e.dma_start(out=cst[:], in_=cst_d))
        fw.dma("pool", s_cstp, [], [WST], lambda e: e.dma_start(out=wsT[:].rearrange("p g t -> p (g t)"), in_=wsT_d))
        fw.dma("pool", s_wpp, [], [WPP], lambda e: e.dma_start(out=wpp[:], in_=kview(w_pp)))

        def cs(name, i=0, n=1):
            o = _CST[name] + i
            return cst[:, o:o + n]

        fw.op("dve", [], [CONSTB], lambda e: e.memset(ones[:], 1.0))
        fw.op("dve", [], [CONSTB], lambda e: e.memset(onesD[:], 1.0 / 1024.0))
        fw.op("dve", [], [CONSTB], lambda e: e.memset(ident[:], 1.0))
        fw.op("pool", [CONSTB], [CONSTB], lambda e: e.affine_select(
            out=ident[:], in_=ident[:], pattern=[[1, 128]], compare_op=ALU.is_equal, fill=0.0,
            base=0, channel_multiplier=-1))
        fw.op("dve", [], [CONSTB], lambda e: e.memset(mask4[:], 1.0))
        fw.op("pool", [CONSTB], [CONSTB], lambda e: e.affine_select(
            out=mask4[:], in_=mask4[:], pattern=[[0, 4], [1, 128]], compare_op=ALU.is_ge, fill=0.0,
            base=0, channel_multiplier=-1))
        fw.op("dve", [CONSTB], [CONSTB], lambda e: e.memset(mask4[0:64, :, 64:128], 0.0))
        fw.op("dve", [], [CONSTB], lambda e: e.memset(msk[:], 1.0))
        fw.op("dve", [CONSTB], [CONSTB], lambda e: e.memset(
            msk[:].rearrange("p (c s) -> p c s", s=64)[:, :, 0:1], 0.0))
        fw.op("dve", [WST], [WST], lambda e: e.memset(wsT[64:128, :, 0:64], 0.0))
        tb1, TB1 = T32.next()
        tb2, TB2 = T32.next()
        bhi = tbf[:].rearrange("p a t -> p (a t)")[:, 0:1024]
        BHI = [b for (_, b) in TBF.items[0:2]]
        blo = t32[:].rearrange("p a t -> p (a t)")[:, 0:1024]
        fw.op("dve", [], [BSP], lambda e: e.memset(bsp[:], 0.0))
        bsf = t32[:].rearrange("p a t -> p (a t)")[:, 2048:3072]
        BSF = [T32.items[4][1], T32.items[5][1]]
        s_bs = sem("d_bs")
        fw.dma("sp", s_bs, [], BSF, lambda e: e.dma_start(out=bsf, in_=bsb_d))
        fw.op("dve", BSF, BHI, lambda e: e.tensor_copy(out=bhi, in_=bsf))
        fw.op("dve", BSF + BHI, [TB1, TB2], lambda e: e.tensor_tensor(out=blo, in0=bsf, in1=bhi, op=ALU.subtract))
        fw.op("dve", BHI + [BSP], [BSP], lambda e: e.tensor_copy(out=bsp[0:1, :], in_=bhi[0:1, :]))
        fw.op("dve", [TB1, TB2, BSP], [BSP], lambda e: e.tensor_copy(out=bsp[32:33, :], in_=blo[32:33, :]))
        fw.op("dve", [CST], [LB], lambda e: e.tensor_tensor(out=ldt[:], in0=cs("l0", 0, 8), in1=cs("l1", 0, 8), op=ALU.subtract))
        fw.op("act", [LB], [LB], lambda e: e.activation(out=lbt[:], in_=ldt[:], func=AF.Sigmoid))
        fw.op("act", [LB], [LB], lambda e: e.activation(out=omlt[:], in_=ldt[:], func=AF.Sigmoid, scale=-1.0))
        fw.op("pool", [], S32, lambda e: e.memset(s32[:], 0.0))
        fw.op("pool", [], CY, lambda e: e.memset(cy[:], 0.0))
        fw.op("pool", [], [b for (_, _, b) in KHABs], lambda e: e.memset(khA[:], 0.0))
        fw.op("pool", [], [b for (_, _, b) in KHABs], lambda e: e.memset(khB[:], 0.0))

        win_v = kview(w_in)
        wa_v = kview(w_a)
        wb_v = kview(w_b)
        wo_v = kview(w_o)
        wup_v = kview(w_up)
        wdn_v = kview(w_dn)
        wpg_v = kview(w_pg)

        def mm8(e, out, slot, j, rhs3, nk=KC):
            last = None
            for k in range(nk):
                last = e.matmul(out, lhsT=slot[:, k, j * 128:(j + 1) * 128], rhs=rhs3[:, k, :],
                                start=(k == 0), stop=(k == nk - 1))
            return last

        def ln_finalize(mb, MB, eb, EB):
            msq, MSQ = T32.next()
            fw.op("act", [MB], [MSQ], lambda e: e.activation(out=msq, in_=mb[:], func=AF.Square))
            var, VAR = T32.next()
            fw.op("dve", [EB, MSQ], [VAR], lambda e: e.tensor_tensor(out=var, in0=eb[:], in1=msq, op=ALU.subtract))
            fw.op("act", [CONSTB, VAR], [VAR], lambda e: e.activation(out=var, in_=var, func=AF.Ln, bias=epsln[:], scale=1.0))
            fw.op("act", [VAR], [VAR], lambda e: e.activation(out=var, in_=var, func=AF.Exp, scale=-0.5))
            nm, NM = T32.next()
            fw.op("dve", [MB, VAR], [NM], lambda e: e.scalar_tensor_tensor(
                out=nm, in0=mb[:], scalar=-1.0, in1=var, op0=ALU.mult, op1=ALU.mult))
            return var, VAR, nm, NM

        epsln = sb("epsln", [128, 1], F32)
        epsrms = sb("epsrms", [128, 1], F32)
        fw.op("dve", [], [CONSTB], lambda e: e.memset(epsln[:], LN_EPS))
        fw.op("dve", [], [CONSTB], lambda e: e.memset(epsrms[:], RMS_EPS))

        out_toks = []

        def prefetch_x(tn):
            if tn >= NT:
                return
            q = tn % 2
            tsl_ = slice(tn * TT, tn * TT + TT)
            fw.dma("pool", s_x[q], [], XBs[q], lambda e: e.dma_start(out=xbfs[q][:], in_=xTv[:, :, tsl_]))

        def tile_gen(tt):
            PLE = "pool" if tt > 0 else "dve"
            xbf = xbfs[tt % 2]
            XB = XBs[tt % 2]
            t0 = tt * TT
            tsl = slice(t0, t0 + TT)

            def fm_units(gi, evac, banks):
                units = []
                hold = {}
                for cbk in range(4):
                    for j in range(2):
                        def unit(cbk=cbk, j=j):
                            if j == 0:
                                c0 = gi * 1024 + cbk * 256
                                hold[cbk] = wload(f"win{gi}_{cbk}", win_v[:, :, c0:c0 + 256], KC, 256, tt)
                            sl, SL = hold[cbk]
                            bank, BANK = banks.next()
                            fw.op("pe", [SL] + XB, [BANK], lambda e, bank=bank, sl=sl, j=j: mm8(e, bank[:], sl, j, xbf))
                            evac(cbk * 2 + j, bank, BANK)
                        units.append(unit)
                return units

            def tm_units(gi, evac, banks):
                units = []
                hold = {}
                for cbk in range(4):
                    for tp in range(2):
                        def unit(cbk=cbk, tp=tp):
                            if tp == 0:
                                c0 = gi * 1024 + cbk * 256
                                hold[cbk] = wload(f"win{gi}_{cbk}", win_v[:, :, c0:c0 + 256], KC, 256, tt)
                            sl, SL = hold[cbk]
                            bank, BANK = banks.next()

                            def f(e, bank=bank, sl=sl, tp=tp):
                                last = None
                                for ti in range(2):
                                    tb = tp * 2 + ti
                                    for k in range(KC):
                                        last = e.matmul(bank[:, ti * 256:(ti + 1) * 256],
                                                        lhsT=xbf[:, k, tb * 128:(tb + 1) * 128], rhs=sl[:, k, :],
                                                        start=(k == 0), stop=(k == KC - 1))
                                return last
                            fw.op("pe", [SL] + XB, [BANK], f)
                            evac(cbk, tp, bank, BANK)
                        units.append(unit)
                return units

            def ev_u(cc, bank, BANK):
                fw.op("act", [BANK], [G[cc]], lambda e: e.activation(out=arena[:, cc, :], in_=bank[:], func=AF.Gelu_apprx_tanh))

            def ev_v(cbk, tp, bank, BANK):
                fw.op("act", [BANK], [R[4 * tp + 0], R[4 * tp + 1], R[4 * tp + 2], R[4 * tp + 3]],
                      lambda e: e.activation(out=vst[:, 2 * tp:2 * tp + 2, cbk * 256:(cbk + 1) * 256],
                                             in_=bank[:].rearrange("p (a c) -> p a c", c=256),
                                             func=AF.Gelu_apprx_tanh))

            def ev_q(cc, bank, BANK):
                fw.op("act", [BANK], [G[8 + cc]], lambda e: e.activation(out=arena[:, 8 + cc, :], in_=bank[:], func=AF.Silu))

            def ev_og(cc, bank, BANK):
                fw.op("act", [BANK], [G[16 + cc]], lambda e: e.activation(out=arena[:, 16 + cc, :], in_=bank[:], func=AF.Silu))

            def ev_i(cbk, tp, bank, BANK):
                fw.op("act", [BANK], [ITM[2 * tp], ITM[2 * tp + 1]],
                      lambda e: e.activation(out=itm[:, 2 * tp:2 * tp + 2, cbk * 256:(cbk + 1) * 256],
                                             in_=bank[:].rearrange("p (a c) -> p a c", c=256), func=AF.Copy))

            def ev_ga(cc, bank, BANK):
                fw.op("act", [BANK], [SGA[cc]], lambda e: e.activation(out=sga[:, cc, :], in_=bank[:], func=AF.Sigmoid))

            def ev_gb(cc, bank, BANK):
                fw.op("act", [BANK], [SGB[cc]], lambda e: e.activation(out=sgb[:, cc, :], in_=bank[:], func=AF.Sigmoid))

            yield tm_units(4, ev_i, FFB)
            yield fm_units(2, ev_q, FFB) + fm_units(5, ev_og, FFB)
            prefetch_x(tt + 1)
            fw.dma("pool", s_p, [], [PB], lambda e: e.dma_start(out=pbf[:], in_=pTv[:, :, tsl]))

            def sgu_S0():
                for tb in range(4):
                    Rb = [R[2 * tb], R[2 * tb + 1]]
                    for hh in range(2):
                        fw.op("dve", Rb, [LNS[tb]], lambda e, hh=hh, tb=tb: e.bn_stats(out=bnst[:, tb, hh, :], in_=vst[:, tb, hh * 512:(hh + 1) * 512]))
                    fw.op("dve", [LNS[tb]], [LNS[tb]], lambda e, tb=tb: e.bn_aggr(out=mv[:, tb, :], in_=bnst[:, tb].rearrange("p a s -> p (a s)")))

            def sgu_S1():
                fw.op("act", LNS + [CONSTB], LNS, lambda e: e.activation(out=rsv[:], in_=mv[:, :, 1], func=AF.Ln, bias=epsln[:], scale=1.0))
                fw.op("act", LNS, LNS, lambda e: e.activation(out=rsv[:], in_=rsv[:], func=AF.Exp, scale=-0.5))
                fw.op("dve", LNS, LNS, lambda e: e.scalar_tensor_tensor(out=nmv[:], in0=mv[:, :, 0], scalar=-1.0, in1=rsv[:], op0=ALU.mult, op1=ALU.mult))

            def sgu_S2():
                for tb in range(4):
                    Rb = [R[2 * tb], R[2 * tb + 1]]
                    fw.op("act", Rb + [LNS[tb]], Rb, lambda e, tb=tb: e.activation(
                        out=vst[:, tb, :], in_=vst[:, tb, :], func=AF.Identity, bias=nmv[:, tb:tb + 1], scale=rsv[:, tb:tb + 1]))

            def sgu_S3(tbs):
                for tb in tbs:
                    Rb = [R[2 * tb], R[2 * tb + 1]]
                    fw.op("dve", Rb + [CST], Rb, lambda e, tb=tb: e.tensor_tensor(out=vst[:, tb, :], in0=vst[:, tb, :], in1=cs("gv", 0, 1024), op=ALU.mult))
                    fw.op("dve", Rb + [CST], [VN[tb]], lambda e, tb=tb: e.tensor_tensor(out=vn[:, tb, :], in0=vst[:, tb, :], in1=cs("bv", 0, 1024), op=ALU.add))

            def sgu_unit(tb):
                for gq in range(2):
                    bank, BANK = FMM.next()

                    def f(e, bank=bank, gq=gq):
                        last = None
                        for gi in range(4):
                            g = gq * 4 + gi
                            e.matmul(bank[:, gi * 128:(gi + 1) * 128], lhsT=vn[:, tb, g * 128:(g + 1) * 128],
                                     rhs=wsT[:, g, :], start=True, stop=False)
                            last = e.matmul(bank[:, gi * 128:(gi + 1) * 128], lhsT=ones[:, :],
                                            rhs=bsp[:, g * 128:(g + 1) * 128], start=False, stop=True)
                        return last
                    fw.op("pe", [VN[tb], WST, BSP, CONSTB], [BANK], f)
                    Gq = G[gq * 4:gq * 4 + 4]
                    fw.op("dve", [BANK] + Gq, Gq, lambda e, bank=bank, gq=gq: e.tensor_tensor(
                        out=arena[:, gq * 4:gq * 4 + 4, tb * 128:(tb + 1) * 128],
                        in0=bank[:].rearrange("p (a t) -> p a t", t=128),
                        in1=arena[:, gq * 4:gq * 4 + 4, tb * 128:(tb + 1) * 128], op=ALU.mult))

            nop = lambda: None

            def hgrn_E0(p):
                if p >= 4:
                    return
                sl, SL = wload(f"win3_{p}", win_v[:, :, 3072 + p * 256:3072 + p * 256 + 256], KC, 256, tt)
                for s in range(2):
                    xb_, XB_ = XBK[s]
                    fw.op("pe", [SL] + XB, [XB_], lambda e, xb_=xb_, s=s, sl=sl: mm8(e, xb_[:], sl, s, xbf))

            def hgrn_head(p, fill, tails, hooks):
                hs = (2 * p, 2 * p + 1)
                for s, h in enumerate(hs):
                    xb_, XB_ = XBK[s]
                    tA, TA = FT[s][0]
                    tB, TB = FT[s][1]
                    fw.op("act", [XB_], [TA], lambda e, tA=tA, xb_=xb_: e.activation(out=tA, in_=xb_[:], func=AF.Sigmoid))
                    fw.op("act", [XB_], [TB], lambda e, tB=tB, xb_=xb_: e.activation(out=tB, in_=xb_[:], func=AF.Sigmoid, scale=-1.0))
                hooks.get("E1", nop)()
                for s, h in enumerate(hs):
                    tA, TA = FT[s][0]
                    fw.op("act", [TA, LB], [TA], lambda e, tA=tA, h=h: e.activation(
                        out=tA, in_=tA, func=AF.Identity, bias=lbt[:, h:h + 1], scale=omlt[:, h:h + 1]))
                for s, h in enumerate(hs):
                    tA, TA = FT[s][0]
                    fw.op("act", [TA], [TA], lambda e, tA=tA: e.activation(out=tA, in_=tA, func=AF.Ln))
                for s, h in enumerate(hs):
                    tA, TA = FT[s][0]
                    tC, TC = FT[s][2]
                    fw.op("dve", [TA, CONSTB], [TC], lambda e, tA=tA, tC=tC: e.tensor_tensor_scan(
                        out=tC, data0=msk[:], data1=tA, initial=0.0, op0=ALU.mult, op1=ALU.add))
                hooks.get("E4", nop)()
                for s, h in enumerate(hs):
                    tA, TA = FT[s][0]
                    tC, TC = FT[s][2]
                    fw.op("act", [TC], [TA], lambda e, tA=tA, tC=tC: e.activation(out=tA, in_=tC, func=AF.Exp))
                    fw.op("act", [TC], [TC], lambda e, tC=tC: e.activation(out=tC, in_=tC, func=AF.Exp, scale=-1.0))
                for s, h in enumerate(hs):
                    tA, TA = FT[s][0]
                    tB, TB = FT[s][1]
                    tC, TC = FT[s][2]
                    kh, KH = KHTs[s]
                    fw.op("dve", [G[8 + h], TA], [G[8 + h]], lambda e, tA=tA, h=h: e.tensor_tensor(
                        out=arena[:, 8 + h, :], in0=arena[:, 8 + h, :], in1=tA, op=ALU.mult))
                    fw.op("dve", [TB, LB, TC], [TB], lambda e, tB=tB, tC=tC, h=h: e.scalar_tensor_tensor(
                        out=tB, in0=tB, scalar=omlt[:, h:h + 1], in1=tC, op0=ALU.mult, op1=ALU.mult))
                    fw.op("dve", [TA], [EL[h]], lambda e, tA=tA, h=h: e.tensor_copy(
                        out=el[:, h, :], in_=tA.rearrange("p (c s) -> p c s", s=64)[:, :, 63]))
                    fw.op("dve", [TB, EL[h]], [KH], lambda e, tB=tB, kh=kh, h=h: e.tensor_tensor(
                        out=kh.rearrange("p (c s) -> p c s", s=64), in0=tB.rearrange("p (c s) -> p c s", s=64),
                        in1=el[:, h, :].unsqueeze(2).to_broadcast([128, 8, 64]), op=ALU.mult))
                tails[0]()
                tails[1]()
                hooks.get("mid", nop)()
                for s, h in enumerate(hs):
                    tB, TB = FT[s][1]
                    kh, KH = KHTs[s]
                    pt, PT_ = PTRs[s]
                    fw.op("act", [TB], [KT[h]], lambda e, tB=tB, h=h: e.activation(out=ktT[:, h, :], in_=tB, func=AF.Copy))
                    fw.op("pe", [KH, CONSTB], [PT_], lambda e, kh=kh, pt=pt: [e.transpose(
                        out=pt[:, b * 128:(b + 1) * 128], in_=kh[:, b * 128:(b + 1) * 128], identity=ident[:]) for b in range(4)][-1])
                for s, h in enumerate(hs):
                    pt, PT_ = PTRs[s]
                    ka, kb, KAB = KHABs[s]
                    fw.op("act", [PT_], [KAB], lambda e, ka=ka, pt=pt: e.activation(
                        out=ka[0:64], in_=pt[0:64, :].rearrange("p (b c) -> p b c", c=128), func=AF.Copy))
                    fw.op("dve", [PT_, KAB], [KAB], lambda e, kb=kb, pt=pt: e.tensor_copy(
                        out=kb[64:128], in_=pt[64:128, :].rearrange("p (b c) -> p b c", c=128)))
                hooks.get("E7", nop)()
                fill[0]()
                fill[1]()
                hooks.get("E8", nop)()
                for s, h in enumerate(hs):
                    xb_, XB_ = XBK[s]
                    ka, kb, KAB = KHABs[s]
                    at, AT_ = ATTs[s]
                    sbh, SBH = SBFs[s]
                    fw.op("pe", [KT[h], G[8 + h]], [XB_], lambda e, xb_=xb_, h=h: [e.matmul(
                        xb_[:, b * 128:(b + 1) * 128], lhsT=ktT[:, h, b * 128:(b + 1) * 128],
                        rhs=arena[:, 8 + h, b * 128:(b + 1) * 128], start=True, stop=True) for b in range(4)][-1])
                    for half in range(2):
                        dbk, DBK = DBK2[half]

                        def f(e, dbk=dbk, half=half, ka=ka, kb=kb, h=h):
                            last = None
                            for ci in range(4):
                                c = half * 4 + ci
                                blk = c // 2
                                src = ka if (c % 2 == 0) else kb
                                last = e.matmul(dbk[:, ci * 128:(ci + 1) * 128], lhsT=src[:, blk, :],
                                                rhs=itm[:, blk, h * 128:(h + 1) * 128], start=True, stop=True)
                            return last
                        fw.op("pe", [KAB] + ITM, [DBK], f)
                    fw.op("dve", [XB_, CONSTB], [AT_], lambda e, at=at, xb_=xb_: e.tensor_tensor(
                        out=at, in0=xb_[:], in1=mask4[:].rearrange("p b t -> p (b t)"), op=ALU.mult))
                    fw.op(PLE, [S32[h], SH], [SH], lambda e, h=h: e.tensor_copy(out=sh[:, 0, :], in_=s32[:, h, :]))
                    for c in range(8):
                        dbk, DBK = DBK2[c // 4]
                        ci = c % 4
                        fw.op("dve", [SH, EL[h], DBK], [SH], lambda e, c=c, dbk=dbk, ci=ci, h=h: e.scalar_tensor_tensor(
                            out=sh[:, c + 1, :], in0=sh[:, c, :], scalar=el[:, h, c:c + 1],
                            in1=dbk[:, ci * 128:(ci + 1) * 128], op0=ALU.mult, op1=ALU.add))
                    fw.op(PLE, [SH], [S32[h]], lambda e, h=h: e.tensor_copy(out=s32[:, h, :], in_=sh[:, 8, :]))
                    fw.op("act", [SH], [SBH], lambda e, sbh=sbh: e.activation(out=sbh, in_=sh[:, 0:8, :], func=AF.Copy))
                    if s == 0:
                        fill[2]()
                        fill[3]()
                        fill[4]()
                fill[5]()
                hgrn_E0(p + 1)
                fill[6]()
                fill[7]()
                tails[2]()

            def hgrn_tail(p):
                hs = (2 * p, 2 * p + 1)

                def T0():
                  for s, h in enumerate(hs):
                    at, AT_ = ATTs[s]
                    sbh, SBH = SBFs[s]
                    ob, OB = OBK
                    tO, TO = FT[s][3]

                    def fo(e, at=at, sbh=sbh, h=h):
                        last = None
                        for b in range(4):
                            e.matmul(ob[:, b * 128:(b + 1) * 128], lhsT=itm[:, b, h * 128:(h + 1) * 128],
                                     rhs=at[:, b * 128:(b + 1) * 128], start=True, stop=False)
                            for hf in range(2):
                                c = 2 * b + hf
                                cs_ = slice(b * 128 + hf * 64, b * 128 + hf * 64 + 64)
                                last = e.matmul(ob[:, cs_], lhsT=sbh[:, c, :], rhs=arena[:, 8 + h, cs_],
                                                start=False, stop=(hf == 1))
                        return last
                    fw.op("pe", ITM + [AT_, SBH, G[8 + h]], [OB], fo)
                    sq, SQ = TBF.next()
                    fw.op("act", [OB], [SQ], lambda e, sq=sq: e.activation(out=sq, in_=ob[:], func=AF.Square))
                    fw.op("act", [OB, CST], [TO], lambda e, tO=tO, h=h: e.activation(
                        out=tO, in_=ob[:], func=AF.Identity, scale=cs("hg", h, 1)))
                    xb_, XB_ = XBK[s]
                    fw.op("pe", [SQ, CONSTB], [XB_], lambda e, xb_=xb_, sq=sq: e.matmul(xb_[:], lhsT=ones[:], rhs=sq, start=True, stop=True))

                def T1():
                  for s, h in enumerate(hs):
                    xb_, XB_ = XBK[s]
                    rs, RS = FT[s][4]
                    fw.op("act", [XB_, CONSTB], [RS], lambda e, rs=rs, xb_=xb_: e.activation(out=rs, in_=xb_[:], func=AF.Ln, bias=epsrms[:], scale=1.0 / 128.0))
                    fw.op("act", [RS], [RS], lambda e, rs=rs: e.activation(out=rs, in_=rs, func=AF.Exp, scale=-0.5))

                def T2():
                  for s, h in enumerate(hs):
                    tO, TO = FT[s][3]
                    rs, RS = FT[s][4]
                    fw.op("dve", [TO, RS], [RS], lambda e, rs=rs, tO=tO: e.tensor_tensor(out=rs, in0=tO, in1=rs, op=ALU.mult))
                    fw.op("dve", [RS, G[16 + h]], [KT[h]], lambda e, rs=rs, h=h: e.tensor_tensor(
                        out=ktT[:, h, :], in0=rs, in1=arena[:, 16 + h, :], op=ALU.mult))
                return (T0, T1, T2)

            fill_v = tm_units(1, ev_v, FMM)
            fill_u = fm_units(0, ev_u, FMM)
            fill_ga = fm_units(6, ev_ga, FMM)
            fill_gb = fm_units(7, ev_gb, FMM)

            fills = [fill_u, fill_v, fill_ga, fill_gb]
            hk = [{}, {},
                  {"E1": sgu_S0, "E4": sgu_S1, "mid": sgu_S2},
                  {"E1": lambda: sgu_S3((0, 1)), "E4": lambda: sgu_S3((2, 3)),
                   "E7": lambda: (sgu_unit(0), sgu_unit(1)), "E8": lambda: (sgu_unit(2), sgu_unit(3))}]
            tails = (nop, nop, nop)
            hgrn_E0(0)
            for p in range(4):
                hgrn_head(p, fills[p], tails, hk[p])
                tails = hgrn_tail(p)
            for t_ in tails:
                t_()

            for dp in range(4):
                c0 = dp * 256
                sa, SA = wload(f"wa{dp}", wa_v[:, :, c0:c0 + 256], KC, 256, tt)
                sbw, SBW = wload(f"wb{dp}", wb_v[:, :, c0:c0 + 256], KC, 256, tt)
                t1s = []
                for j in range(2):
                    dj = dp * 2 + j
                    ba, BA = FFB.next()
                    fw.op("pe", [SA] + G[0:8], [BA], lambda e, ba=ba, sa=sa, j=j: mm8(e, ba[:], sa, j, arena[:, 0:8, :]))
                    t1, T1 = T32.next()
                    fw.op("dve", [BA, SGA[dj]], [T1], lambda e, t1=t1, ba=ba, dj=dj: e.tensor_tensor(out=t1, in0=ba[:], in1=sga[:, dj, :], op=ALU.mult))
                    t1s.append((t1, T1))
                for j in range(2):
                    dj = dp * 2 + j
                    t1, T1 = t1s[j]
                    bb, BB = FFB.next()
                    fw.op("pe", [SBW] + KT, [BB], lambda e, bb=bb, sbw=sbw, j=j: mm8(e, bb[:], sbw, j, ktT))
                    t2, T2 = T32.next()
                    fw.op("dve", [BB, SGB[dj]], [T2], lambda e, t2=t2, bb=bb, dj=dj: e.tensor_tensor(out=t2, in0=bb[:], in1=sgb[:, dj, :], op=ALU.mult))
                    fw.op("dve", [T1, T2], [VN[dj // 2]], lambda e, t1=t1, t2=t2, dj=dj: e.tensor_tensor(out=mT[:, dj, :], in0=t1, in1=t2, op=ALU.add))

            fw.dma("pool", s_r, [], R, lambda e, tsl=tsl: e.dma_start(out=resid[:], in_=xTv[:, :, tsl]))

            def layer_norm_fm(gname, bname, write_bf):
                pass

            def stats_accum(dj, mb, MB, eb, EB):
                yb, YB = TBF.next()
                fw.op("act", [R[dj]], [YB], lambda e: e.activation(out=yb, in_=resid[:, dj, :], func=AF.Copy))
                ys, YS = TBF.next()
                fw.op("act", [R[dj]], [YS], lambda e: e.activation(out=ys, in_=resid[:, dj, :], func=AF.Square))

                def pe_part():
                    fw.op("pe", [YB, CONSTB], [MB], lambda e: e.matmul(mb[:], lhsT=onesD[:], rhs=yb, start=(dj == 0), stop=(dj == KC - 1)))
                    fw.op("pe", [YS, CONSTB], [EB], lambda e: e.matmul(eb[:], lhsT=onesD[:], rhs=ys, start=(dj == 0), stop=(dj == KC - 1)))
                return pe_part

            def normalize(gname, bname, rs, RS, nm, NM, bf_out):
                for dj in range(KC):
                    normalize_chunk(dj, gname, bname, rs, RS, nm, NM, bf_out)

            def normalize_chunk(dj, gname, bname, rs, RS, nm, NM, bf_out):
                if True:
                    fw.op("dve", [R[dj], RS], [R[dj]], lambda e, dj=dj: e.tensor_tensor(out=resid[:, dj, :], in0=resid[:, dj, :], in1=rs, op=ALU.mult))
                    fw.op(PLE, [R[dj], NM], [R[dj]], lambda e, dj=dj: e.tensor_tensor(out=resid[:, dj, :], in0=resid[:, dj, :], in1=nm, op=ALU.add))
                    if bf_out:
                        fw.op("act", [R[dj], CST], [XB[dj]], lambda e, dj=dj: e.activation(
                            out=xbf[:, dj, :], in_=resid[:, dj, :], func=AF.Identity,
                            bias=cs(bname, dj, 1), scale=cs(gname, dj, 1)))
                    fw.op("act", [R[dj], CST], [R[dj]], lambda e, dj=dj: e.activation(
                        out=resid[:, dj, :], in_=resid[:, dj, :], func=AF.Identity,
                        bias=cs(bname, dj, 1), scale=cs(gname, dj, 1)))


            mb, MB = AUX.next()
            eb, EB = AUX.next()
            pend = None
            for dp in range(4):
                c0 = dp * 256
                so, SO = wload(f"wo{dp}", wo_v[:, :, c0:c0 + 256], KC, 256, tt)
                for j in range(2):
                    dj = dp * 2 + j
                    bo, BO = MM.next()
                    fw.op("pe", [SO] + VN, [BO], lambda e, bo=bo, so=so, j=j: mm8(e, bo[:], so, j, mT))
                    if pend is not None:
                        pend()
                    fw.op("dve", [R[dj], BO], [R[dj]], lambda e, dj=dj, bo=bo: e.scalar_tensor_tensor(
                        out=resid[:, dj, :], in0=resid[:, dj, :], scalar=ALPHA, in1=bo[:], op0=ALU.mult, op1=ALU.add))
                    pend = stats_accum(dj, mb, MB, eb, EB)
            pend()
            yield
            st1 = {}

            def ln1_fin(mb=mb, MB=MB, eb=eb, EB=EB):
                st1["v"] = ln_finalize(mb, MB, eb, EB)

            def ln1_chunk(dj):
                rs, RS, nm, NM = st1["v"]
                normalize_chunk(dj, "ln1g", "ln1b", rs, RS, nm, NM, True)
            yield (ln1_fin, ln1_chunk)

            for fp in range(FC // 2):
                c0 = fp * 256
                sg_, SG_ = wload(f"wug{fp}", wup_v[:, :, c0:c0 + 256], KC, 256, tt)
                sv_, SV_ = wload(f"wuv{fp}", wup_v[:, :, DFF + c0:DFF + c0 + 256], KC, 256, tt)
                for j in range(2):
                    fc = fp * 2 + j
                    bg, BG = FFB.next()
                    fw.op("pe", [SG_] + XB, [BG], lambda e, bg=bg, sg_=sg_, j=j: mm8(e, bg[:], sg_, j, xbf))
                    bv, BV = FFB.next()
                    fw.op("pe", [SV_] + XB, [BV], lambda e, bv=bv, sv_=sv_, j=j: mm8(e, bv[:], sv_, j, xbf))
                    g_, GS_ = GS.next()
                    fw.op(PLE, [CY[fc], GS_], [GS_], lambda e, g_=g_, fc=fc: e.tensor_copy(out=g_[:, 0:2], in_=cy[:, fc, :]))
                    fw.op("act", [BG, GS_], [GS_], lambda e, g_=g_, bg=bg: e.activation(out=g_[:, 2:TT + 2], in_=bg[:], func=AF.Copy))
                    fw.op(PLE, [GS_, CY[fc]], [CY[fc]], lambda e, g_=g_, fc=fc: e.tensor_copy(out=cy[:, fc, :], in_=g_[:, TT:TT + 2]))
                    ac, AC = T32.next()
                    cw0 = _CST["cw"] + fc * 3
                    fw.op("act", [BG, CST], [AC], lambda e, ac=ac, bg=bg, cw0=cw0, fc=fc: e.activation(
                        out=ac, in_=bg[:], func=AF.Identity, bias=cs("cb", fc, 1), scale=cst[:, cw0 + 2:cw0 + 3]))
                    fw.op("dve", [GS_, CST, AC], [AC], lambda e, ac=ac, g_=g_, cw0=cw0: e.scalar_tensor_tensor(
                        out=ac, in0=g_[:, 1:TT + 1], scalar=cst[:, cw0 + 1:cw0 + 2], in1=ac, op0=ALU.mult, op1=ALU.add))
                    fw.op("dve", [GS_, CST, AC], [AC], lambda e, ac=ac, g_=g_, cw0=cw0: e.scalar_tensor_tensor(
                        out=ac, in0=g_[:, 0:TT], scalar=cst[:, cw0:cw0 + 1], in1=ac, op0=ALU.mult, op1=ALU.add))
                    fw.op("act", [AC], [AC], lambda e, ac=ac: e.activation(out=ac, in_=ac, func=AF.Gelu_apprx_tanh))
                    fw.op("dve", [AC, BV], [G[fc]], lambda e, ac=ac, bv=bv, fc=fc: e.tensor_tensor(
                        out=arena[:, fc, :], in0=bv[:], in1=ac, op=ALU.mult))

            mb, MB = AUX.next()
            eb, EB = AUX.next()
            pend = None
            for dp in range(4):
                sp_, SP_ = wload(f"wpg{dp}", wpg_v[:, :, dp * 256:dp * 256 + 256], KC, 256, tt)
                for j in range(2):
                    dj = dp * 2 + j
                    sd, SD = wload(f"wdn{dj}", wdn_v[:, :, dj * 128:dj * 128 + 128], FC, 128, tt)
                    bf_, BF_ = MM.next()
                    fw.op("pe", [SD] + G[0:FC], [BF_], lambda e, bf_=bf_, sd=sd: mm8(e, bf_[:], sd, 0, arena[:, 0:FC, :], nk=FC))
                    bpg, BPG = MM.next()
                    fw.op("pe", [SP_] + XB, [BPG], lambda e, bpg=bpg, sp_=sp_, j=j: mm8(e, bpg[:], sp_, j, xbf))
                    bpp, BPP = MM.next()
                    fw.op("pe", [WPP, PB], [BPP], lambda e, bpp=bpp, dj=dj: mm8(e, bpp[:], wpp[:, :, dj * 128:(dj + 1) * 128], 0, pbf, nk=2))
                    if pend is not None:
                        pend()
                    sgp, SGP = TBF.next()
                    fw.op("act", [BPG], [SGP], lambda e, sgp=sgp, bpg=bpg: e.activation(out=sgp, in_=bpg[:], func=AF.Sigmoid))
                    t1, T1 = T32.next()
                    fw.op("dve", [BPP, SGP], [T1], lambda e, t1=t1, bpp=bpp, sgp=sgp: e.tensor_tensor(out=t1, in0=bpp[:], in1=sgp, op=ALU.mult))
                    fw.op("dve", [R[dj], BF_], [R[dj]], lambda e, dj=dj, bf_=bf_: e.scalar_tensor_tensor(
                        out=resid[:, dj, :], in0=resid[:, dj, :], scalar=ALPHA, in1=bf_[:], op0=ALU.mult, op1=ALU.add))
                    fw.op(PLE, [R[dj], T1], [R[dj]], lambda e, dj=dj, t1=t1: e.tensor_tensor(
                        out=resid[:, dj, :], in0=resid[:, dj, :], in1=t1, op=ALU.add))
                    pend = stats_accum(dj, mb, MB, eb, EB)
            pend()
            yield
            st2 = {}

            def ln2_fin():
                st2["v"] = ln_finalize(mb, MB, eb, EB)

            def ln2_chunk(dj):
                rs, RS, nm, NM = st2["v"]
                normalize_chunk(dj, "ln2g", "ln2b", rs, RS, nm, NM, False)

            def ln2_out():
                out_toks.append(fw.dma("pool", s_o, R, [], lambda e: e.dma_start(out=outTv[:, :, tsl], in_=resid[:])))
            yield (ln2_fin, ln2_chunk, ln2_out)

        gens = [tile_gen(t_) for t_ in range(NT)]
        prefetch_x(0)
        for u_ in next(gens[0]):
            u_()
        for u_ in next(gens[0]):
            u_()
        for t_ in range(NT):
            g_t = gens[t_]
            next(g_t)
            units_i = list(next(gens[t_ + 1])) if t_ + 1 < NT else []
            fin1, chunk1 = next(g_t)
            fin1()
            for dj in range(KC):
                chunk1(dj)
                if dj < len(units_i):
                    units_i[dj]()
            next(g_t)
            units = list(next(gens[t_ + 1])) if t_ + 1 < NT else []
            fin, chunk, outd = next(g_t)
            fin()
            per = (len(units) + KC - 1) // KC if units else 0
            for dj in range(KC):
                chunk(dj)
                for u_ in units[dj * per:(dj + 1) * per]:
                    u_()
            outd()

        fw.final_wait("pool", out_toks[-1])

        block = es.enter_context(nc.Block())

        @block.sync
        def _(e):
            for th in fw.prog["sp"]:
                th(e)

        @block.tensor
        def _(e):
            for th in fw.prog["pe"]:
                th(e)

        @block.scalar
        def _(e):
            for th in fw.prog["act"]:
                th(e)

        @block.vector
        def _(e):
            for th in fw.prog["dve"]:
                th(e)

        @block.gpsimd
        def _(e):
            for th in fw.prog["pool"]:
                th(e)
    return nc


def _pk(v):
    return np.ascontiguousarray(np.asarray(v, np.float32).reshape(-1, 128).T)


def _prep_shared(inp):
    g = lambda k: np.asarray(inp[k], np.float32)
    cst = np.zeros((128, NCST), np.float32)

    def put(name, arr):
        cst[:, _CST[name]:_CST[name] + arr.shape[1]] = arr
    put("ln1g", _pk(g("ln1_g")[0]))
    put("ln1b", _pk(g("ln1_b")[0]))
    put("ln2g", _pk(g("ln2_g")[0]))
    put("ln2b", _pk(g("ln2_b")[0]))
    put("hg", _pk(g("hgrn_norm_g")[0]))
    put("l0", _pk(g("hgrn_lb_logits")[0]))
    put("l1", _pk(g("hgrn_lb_logits")[1]))
    cw = g("ffn_conv_w")[0]
    cwp = np.stack([_pk(cw[j]) for j in range(3)], axis=2)
    put("cw", cwp.reshape(128, 66))
    put("cb", _pk(g("ffn_conv_b")[0]))
    put("gv", np.broadcast_to(g("sgu_norm_g")[0][None, :], (128, 1024)))
    put("bv", np.broadcast_to(g("sgu_norm_b")[0][None, :], (128, 1024)))
    wsT = np.ascontiguousarray(g("sgu_w_s")[0].transpose(2, 0, 1)).reshape(128, 1024)
    return {
        "w_in": np.ascontiguousarray(g("w_in")[0]),
        "w_a": np.ascontiguousarray(g("w_branch")[0, 0]),
        "w_b": np.ascontiguousarray(g("w_branch")[0, 1]),
        "w_o": np.ascontiguousarray(g("w_out")[0]),
        "w_up": np.ascontiguousarray(g("ffn_w_up")[0]),
        "w_dn": np.ascontiguousarray(g("ffn_w_down")[0]),
        "w_pg": np.ascontiguousarray(g("ple_w_gate")[0]),
        "w_pp": np.ascontiguousarray(g("ple_w_proj")[0]),
        "wsT": wsT,
        "cst": cst,
        "bsb": np.ascontiguousarray(np.broadcast_to(g("sgu_b_s")[0].reshape(1, 1024), (128, 1024))),
    }


_NC_CACHE = {}


def kernel(**inputs):
    x = np.asarray(inputs["x"], np.float32)
    p = np.asarray(inputs["p"], np.float32)
    B, T, _ = x.shape
    assert B == NCORES and T % TT == 0
    shared = _prep_shared(inputs)
    in_maps = []
    for b in range(B):
        m = dict(shared)
        m["xT"] = np.ascontiguousarray(x[b].T)
        m["pT"] = np.ascontiguousarray(p[0, b].T)
        in_maps.append(m)
    if T not in _NC_CACHE:
        _NC_CACHE[T] = build(T)
    nc = _NC_CACHE[T]
    res = run_bass_kernel_spmd(nc, in_maps, core_ids=list(range(NCORES)))
    out = np.stack([np.ascontiguousarray(r["outT"].T) for r in res.results], axis=0)
    return out.astype(np.float32)
```

```python
import numpy as np
from contextlib import ExitStack
import concourse.bass as bass
import concourse.mybir as mybir
from concourse.bass_utils import run_bass_kernel_spmd

F32 = mybir.dt.float32
BF16 = mybir.dt.bfloat16
AF = mybir.ActivationFunctionType
ALU = mybir.AluOpType

D = 1024
KC = 8
TT = 512
DFF = 2816
FC = 22
PLE = 256
NCORES = 8
ALPHA = float(2.0 ** 0.25)
LN_EPS = 1e-5
RMS_EPS = 1e-6
SLOT_CAP = 2816
NSLOT = 5

_CST = {}
_o = 0
for _n, _w in (("ln1g", 8), ("ln1b", 8), ("ln2g", 8), ("ln2b", 8), ("hg", 8), ("l0", 8), ("l1", 8),
               ("cw", 66), ("cb", 22), ("gv", 1024), ("bv", 1024)):
    _CST[_n] = _o
    _o += _w
NCST = _o


class Buf:
    __slots__ = ("name", "w", "r", "excl")

    def __init__(self, name, excl=False):
        self.name = name
        self.w = None
        self.r = {}
        self.excl = excl


class FW:
    ENG = ("pe", "act", "dve", "pool", "sp")

    def __init__(self, nc, es):
        self.nc = nc
        self.prog = {e: [] for e in self.ENG}
        self.esem = {e: es.enter_context(nc.semaphore("es_" + e)) for e in ("pe", "act", "dve", "pool")}
        self.cnt = {e: 0 for e in self.ENG}
        self.seen = {e: {} for e in self.ENG}
        self.dcount = {}
        self.nwait = 0

    def _waits(self, eng, reads, writes):
        need = {}

        def add(tok):
            if tok is None:
                return
            sem, val = tok
            k = id(sem)
            if k not in need or need[k][1] < val:
                need[k] = (sem, val)

        for b in reads:
            add(b.w)
            if b.excl:
                for tok in b.r.values():
                    add(tok)
        for b in writes:
            add(b.w)
            for tok in b.r.values():
                add(tok)
        out = []
        for k, (sem, val) in need.items():
            if eng == "pe" and sem is self.esem["pe"]:
                continue
            if self.seen[eng].get(k, 0) >= val:
                continue
            self.seen[eng][k] = val
            out.append((sem, val))
        return out

    def _mark(self, reads, writes, tok):
        k = id(tok[0])
        for b in reads:
            b.r[k] = tok
        for b in writes:
            b.w = tok
            b.r = {}

    def _emit_waits(self, eng, reads, writes):
        for sem, val in self._waits(eng, reads, writes):
            self.nwait += 1
            self.prog[eng].append(lambda e, sem=sem, val=val: e.wait_ge(sem, val))

    def op(self, eng, reads, writes, fn):
        self._emit_waits(eng, reads, writes)
        self.cnt[eng] += 1
        sem = self.esem[eng]
        tok = (sem, self.cnt[eng])
        self.prog[eng].append(lambda e, fn=fn, sem=sem: fn(e).then_inc(sem, 1))
        self._mark(reads, writes, tok)

    def dma(self, q, dsem, reads, writes, fn):
        self._emit_waits(q, reads, writes)
        k = id(dsem)
        self.dcount[k] = self.dcount.get(k, 0) + 16
        tok = (dsem, self.dcount[k])
        self.prog[q].append(lambda e, fn=fn, dsem=dsem: fn(e).then_inc(dsem, 16))
        self._mark(reads, writes, tok)
        return tok

    def final_wait(self, q, tok):
        sem, val = tok
        self.prog[q].append(lambda e, sem=sem, val=val: e.wait_ge(sem, val))


class Rot:
    def __init__(self, items):
        self.items = items
        self.i = 0

    def next(self):
        it = self.items[self.i % len(self.items)]
        self.i += 1
        return it


def build(T):
    NT = T // TT
    nc = bass.Bass("TRN2", target_bir_lowering=False)

    def din(name, shape):
        return nc.dram_tensor(name, shape, F32, kind="ExternalInput").ap()

    xT = din("xT", [D, T])
    pT = din("pT", [PLE, T])
    w_in = din("w_in", [D, 8192])
    w_a = din("w_a", [D, D])
    w_b = din("w_b", [D, D])
    w_o = din("w_o", [D, D])
    w_up = din("w_up", [D, 2 * DFF])
    w_dn = din("w_dn", [DFF, D])
    w_pg = din("w_pg", [D, D])
    w_pp = din("w_pp", [PLE, D])
    wsT_d = din("wsT", [128, 8 * 128])
    cst_d = din("cst", [128, NCST])
    bsb_d = din("bsb", [128, 1024])
    outT = nc.dram_tensor("outT", [D, T], F32, kind="ExternalOutput").ap()

    def kview(w):
        return w.rearrange("(k p) c -> p k c", p=128)

    xTv = kview(xT)
    pTv = kview(pT)
    outTv = kview(outT)

    es = ExitStack()
    with es:
        fw = FW(nc, es)

        def sb(name, shape, dt):
            return es.enter_context(nc.sbuf_tensor("sb_" + name, shape, dt))

        def sem(name):
            return es.enter_context(nc.semaphore(name))

        arena = sb("arena", [128, 24, TT], BF16)
        G = [Buf(f"g{i}") for i in range(24)]
        xbfs = [sb(f"xbf{q}", [128, KC, TT], BF16) for q in range(2)]
        XBs = [[Buf(f"xb{q}_{i}") for i in range(KC)] for q in range(2)]
        resid = sb("resid", [128, KC, TT], F32)
        R = [Buf(f"r{i}") for i in range(KC)]
        vst = resid[:].rearrange("p (b h) t -> p b (h t)", h=2)
        vn = sb("vn", [128, 4, 1024], BF16)
        VN = [Buf(f"vn{i}") for i in range(4)]
        mT = vn[:].rearrange("p b (h t) -> p (b h) t", h=2)
        ktT = sb("ktT", [128, KC, TT], BF16)
        KT = [Buf(f"kt{i}") for i in range(KC)]
        khT = sb("khT", [128, 2, TT], BF16)
        KHTs = [(khT[:, i, :], Buf(f"kht{i}")) for i in range(2)]
        khA = sb("khA", [128, 2, 4, 128], BF16)
        khB = sb("khB", [128, 2, 4, 128], BF16)
        KHABs = [(khA[:, i], khB[:, i], Buf(f"khab{i}")) for i in range(2)]
        itm = sb("itm", [128, 4, 1024], BF16)
        ITM = [Buf(f"itm{i}") for i in range(4)]
        att = sb("att", [128, 2, TT], BF16)
        ATTs = [(att[:, i, :], Buf(f"att{i}")) for i in range(2)]
        s32 = sb("s32", [128, 8, 128], F32)
        S32 = [Buf(f"s32_{i}") for i in range(8)]
        sh = sb("sh", [128, 9, 128], F32)
        SH = Buf("sh")
        sbf = sb("sbf", [128, 2, 8, 128], BF16)
        SBFs = [(sbf[:, i], Buf(f"sbf{i}")) for i in range(2)]
        ft = sb("ft", [128, 2, 5, TT], F32)
        FT = [[(ft[:, s_, k_, :], Buf(f"ft{s_}_{k_}")) for k_ in range(5)] for s_ in range(2)]
        sga = sb("sga", [128, KC, TT], BF16)
        SGA = [Buf(f"sga{i}") for i in range(KC)]
        sgb = sb("sgb", [128, KC, TT], BF16)
        SGB = [Buf(f"sgb{i}") for i in range(KC)]
        NTMP = 6
        t32 = sb("t32", [128, NTMP, TT], F32)
        T32 = Rot([(t32[:, i, :], Buf(f"t32_{i}")) for i in range(NTMP)])
        tbf = sb("tbf", [128, 4, TT], BF16)
        TBF = Rot([(tbf[:, i, :], Buf(f"tbf{i}")) for i in range(4)])
        gs = sb("gs", [128, 2, TT + 2], F32)
        GS = Rot([(gs[:, i, :], Buf(f"gs{i}")) for i in range(2)])
        cy = sb("cy", [128, FC, 2], F32)
        CY = [Buf(f"cy{i}") for i in range(FC)]
        pbf = sb("pbf", [128, 2, TT], BF16)
        PB = Buf("pb")
        wpp = sb("wpp", [128, 2, D], BF16)
        WPP = Buf("wpp")
        cst = sb("cst", [128, NCST], F32)
        CST = Buf("cst")
        wsT = sb("wsT", [128, 8, 128], BF16)
        WST = Buf("wsT")
        bsp = sb("bsp", [128, 1024], BF16)
        BSP = Buf("bsp")
        ones = sb("ones", [128, 128], BF16)
        onesD = sb("onesD", [128, 128], BF16)
        ident = sb("ident", [128, 128], BF16)
        CONSTB = Buf("constb")
        mask4 = sb("mask4", [128, 4, 128], BF16)
        msk = sb("msk", [128, TT], F32)
        el = sb("el", [128, 8, 8], F32)
        EL = [Buf(f"el{i}") for i in range(8)]
        lbt = sb("lbt", [128, 8], F32)
        omlt = sb("omlt", [128, 8], F32)
        ldt = sb("ldt", [128, 8], F32)
        LB = Buf("lb")
        bnst = sb("bnst", [128, 4, 2, 6], F32)
        mv = sb("mv", [128, 4, 2], F32)
        rsv = sb("rsv", [128, 4], F32)
        nmv = sb("nmv", [128, 4], F32)
        LNS = [Buf(f"lns{i}") for i in range(4)]
        wslots = [sb(f"wsl{i}", [128, SLOT_CAP], BF16) for i in range(NSLOT)]
        WS = [Buf(f"ws{i}") for i in range(NSLOT)]

        pmm = [es.enter_context(nc.psum_tensor(f"pmm{i}", [128, TT], F32)) for i in range(4)]
        PMMB = [Buf(f"pmm{i}", True) for i in range(4)]
        MM = Rot([(pmm[i], PMMB[i]) for i in range(4)])
        FMM = Rot([(pmm[i], PMMB[i]) for i in range(2)])
        DBK2 = [(pmm[2], PMMB[2]), (pmm[3], PMMB[3])]
        pax = [es.enter_context(nc.psum_tensor(f"pax{i}", [128, TT], F32)) for i in range(3)]
        PAXB = [Buf(f"pax{i}", True) for i in range(3)]
        AUX = Rot([(pax[i], PAXB[i]) for i in range(3)])
        XBK = [(pax[0], PAXB[0]), (pax[1], PAXB[1])]
        OBK = (pax[2], PAXB[2])
        ptr = es.enter_context(nc.psum_tensor("ptr", [128, 1024], BF16))
        PTRB = Buf("ptr")
        PTRs = [(ptr[:, i * 512:(i + 1) * 512], PTRB) for i in range(2)]

        s_cst = sem("d_cst")
        s_cstp = sem("d_cstp")
        s_wpp = sem("d_wpp")
        s_x = [sem("d_x0"), sem("d_x1")]
        s_p = sem("d_p")
        s_r = sem("d_r")
        s_o = sem("d_o")
        s_wl = [sem(f"d_wl{i}") for i in range(NSLOT)]
        s_ws = [sem(f"d_ws{i}") for i in range(NSLOT)]
        s_wr = [sem(f"d_wr{i}") for i in range(NSLOT)]

        scratch = {}
        slot_rr = [0]

        def wload(key, src_view, kc, cols, tile_idx):
            si = slot_rr[0] % NSLOT
            slot_rr[0] += 1
            sl = wslots[si][:, 0:kc * cols].rearrange("p (k c) -> p k c", c=cols)
            b = WS[si]
            if tile_idx == 0:
                fw.dma("pool", s_wl[si], [], [b], lambda e, sl=sl, src_view=src_view: e.dma_start(out=sl, in_=src_view))
                if NT > 1:
                    dr = nc.dram_tensor("scr_" + key, [128, kc * cols], BF16)
                    db = Buf("scr_" + key)
                    scratch[key] = (dr, db)
                    drv = dr.ap().rearrange("p (k c) -> p k c", c=cols)
                    fw.dma("sp", s_ws[si], [b], [db], lambda e, sl=sl, drv=drv: e.dma_start(out=drv, in_=sl))
            else:
                dr, db = scratch[key]
                drv = dr.ap().rearrange("p (k c) -> p k c", c=cols)
                fw.dma("sp", s_wr[si], [db], [b], lambda e, sl=sl, drv=drv: e.dma_start(out=sl, in_=drv))
            return sl, b

        fw.dma("sp", s_cst, [], [CST], lambda e: e.dma_start(out=cst[:], in_=cst_d))
        fw.dma("pool", s_cstp, [], [WST], lambda e: e.dma_start(out=wsT[:].rearrange("p g t -> p (g t)"), in_=wsT_d))
        fw.dma("pool", s_wpp, [], [WPP], lambda e: e.dma_start(out=wpp[:], in_=kview(w_pp)))

        def cs(name, i=0, n=1):
            o = _CST[name] + i
            return cst[:, o:o + n]

        fw.op("dve", [], [CONSTB], lambda e: e.memset(ones[:], 1.0))
        fw.op("dve", [], [CONSTB], lambda e: e.memset(onesD[:], 1.0 / 1024.0))
        fw.op("dve", [], [CONSTB], lambda e: e.memset(ident[:], 1.0))
        fw.op("pool", [CONSTB], [CONSTB], lambda e: e.affine_select(
            out=ident[:], in_=ident[:], pattern=[[1, 128]], compare_op=ALU.is_equal, fill=0.0,
            base=0, channel_multiplier=-1))
        fw.op("dve", [], [CONSTB], lambda e: e.memset(mask4[:], 1.0))
        fw.op("pool", [CONSTB], [CONSTB], lambda e: e.affine_select(
            out=mask4[:], in_=mask4[:], pattern=[[0, 4], [1, 128]], compare_op=ALU.is_ge, fill=0.0,
            base=0, channel_multiplier=-1))
        fw.op("dve", [CONSTB], [CONSTB], lambda e: e.memset(mask4[0:64, :, 64:128], 0.0))
        fw.op("dve", [], [CONSTB], lambda e: e.memset(msk[:], 1.0))
        fw.op("dve", [CONSTB], [CONSTB], lambda e: e.memset(
            msk[:].rearrange("p (c s) -> p c s", s=64)[:, :, 0:1], 0.0))
        fw.op("dve", [WST], [WST], lambda e: e.memset(wsT[64:128, :, 0:64], 0.0))
        tb1, TB1 = T32.next()
        tb2, TB2 = T32.next()
        bhi = tbf[:].rearrange("p a t -> p (a t)")[:, 0:1024]
        BHI = [b for (_, b) in TBF.items[0:2]]
        blo = t32[:].rearrange("p a t -> p (a t)")[:, 0:1024]
        fw.op("dve", [], [BSP], lambda e: e.memset(bsp[:], 0.0))
        bsf = t32[:].rearrange("p a t -> p (a t)")[:, 2048:3072]
        BSF = [T32.items[4][1], T32.items[5][1]]
        s_bs = sem("d_bs")
        fw.dma("sp", s_bs, [], BSF, lambda e: e.dma_start(out=bsf, in_=bsb_d))
        fw.op("dve", BSF, BHI, lambda e: e.tensor_copy(out=bhi, in_=bsf))
        fw.op("dve", BSF + BHI, [TB1, TB2], lambda e: e.tensor_tensor(out=blo, in0=bsf, in1=bhi, op=ALU.subtract))
        fw.op("dve", BHI + [BSP], [BSP], lambda e: e.tensor_copy(out=bsp[0:1, :], in_=bhi[0:1, :]))
        fw.op("dve", [TB1, TB2, BSP], [BSP], lambda e: e.tensor_copy(out=bsp[32:33, :], in_=blo[32:33, :]))
        fw.op("dve", [CST], [LB], lambda e: e.tensor_tensor(out=ldt[:], in0=cs("l0", 0, 8), in1=cs("l1", 0, 8), op=ALU.subtract))
        fw.op("act", [LB], [LB], lambda e: e.activation(out=lbt[:], in_=ldt[:], func=AF.Sigmoid))
        fw.op("act", [LB], [LB], lambda e: e.activation(out=omlt[:], in_=ldt[:], func=AF.Sigmoid, scale=-1.0))
        fw.op("pool", [], S32, lambda e: e.memset(s32[:], 0.0))
        fw.op("pool", [], CY, lambda e: e.memset(cy[:], 0.0))
        fw.op("pool", [], [b for (_, _, b) in KHABs], lambda e: e.memset(khA[:], 0.0))
        fw.op("pool", [], [b for (_, _, b) in KHABs], lambda e: e.memset(khB[:], 0.0))

        win_v = kview(w_in)
        wa_v = kview(w_a)
        wb_v = kview(w_b)
        wo_v = kview(w_o)
        wup_v = kview(w_up)
        wdn_v = kview(w_dn)
        wpg_v = kview(w_pg)

        def mm8(e, out, slot, j, rhs3, nk=KC):
            last = None
            for k in range(nk):
                last = e.matmul(out, lhsT=slot[:, k, j * 128:(j + 1) * 128], rhs=rhs3[:, k, :],
                                start=(k == 0), stop=(k == nk - 1))
            return last

        def ln_finalize(mb, MB, eb, EB):
            msq, MSQ = T32.next()
            fw.op("act", [MB], [MSQ], lambda e: e.activation(out=msq, in_=mb[:], func=AF.Square))
            var, VAR = T32.next()
            fw.op("dve", [EB, MSQ], [VAR], lambda e: e.tensor_tensor(out=var, in0=eb[:], in1=msq, op=ALU.subtract))
            fw.op("act", [CONSTB, VAR], [VAR], lambda e: e.activation(out=var, in_=var, func=AF.Ln, bias=epsln[:], scale=1.0))
            fw.op("act", [VAR], [VAR], lambda e: e.activation(out=var, in_=var, func=AF.Exp, scale=-0.5))
            nm, NM = T32.next()
            fw.op("dve", [MB, VAR], [NM], lambda e: e.scalar_tensor_tensor(
                out=nm, in0=mb[:], scalar=-1.0, in1=var, op0=ALU.mult, op1=ALU.mult))
            return var, VAR, nm, NM

        epsln = sb("epsln", [128, 1], F32)
        epsrms = sb("epsrms", [128, 1], F32)
        fw.op("dve", [], [CONSTB], lambda e: e.memset(epsln[:], LN_EPS))
        fw.op("dve", [], [CONSTB], lambda e: e.memset(epsrms[:], RMS_EPS))

        out_toks = []

        def prefetch_x(tn):
            if tn >= NT:
                return
            q = tn % 2
            tsl_ = slice(tn * TT, tn * TT + TT)
            fw.dma("pool", s_x[q], [], XBs[q], lambda e: e.dma_start(out=xbfs[q][:], in_=xTv[:, :, tsl_]))

        def tile_gen(tt):
            PLE = "pool" if tt > 0 else "dve"
            xbf = xbfs[tt % 2]
            XB = XBs[tt % 2]
            t0 = tt * TT
            tsl = slice(t0, t0 + TT)

            def fm_units(gi, evac, banks):
                units = []
                hold = {}
                for cbk in range(4):
                    for j in range(2):
                        def unit(cbk=cbk, j=j):
                            if j == 0:
                                c0 = gi * 1024 + cbk * 256
                                hold[cbk] = wload(f"win{gi}_{cbk}", win_v[:, :, c0:c0 + 256], KC, 256, tt)
                            sl, SL = hold[cbk]
                            bank, BANK = banks.next()
                            fw.op("pe", [SL] + XB, [BANK], lambda e, bank=bank, sl=sl, j=j: mm8(e, bank[:], sl, j, xbf))
                            evac(cbk * 2 + j, bank, BANK)
                        units.append(unit)
                return units

            def tm_units(gi, evac, banks):
                units = []
                hold = {}
                for cbk in range(4):
                    for tp in range(2):
                        def unit(cbk=cbk, tp=tp):
                            if tp == 0:
                                c0 = gi * 1024 + cbk * 256
                                hold[cbk] = wload(f"win{gi}_{cbk}", win_v[:, :, c0:c0 + 256], KC, 256, tt)
                            sl, SL = hold[cbk]
                            bank, BANK = banks.next()

                            def f(e, bank=bank, sl=sl, tp=tp):
                                last = None
                                for ti in range(2):
                                    tb = tp * 2 + ti
                                    for k in range(KC):
                                        last = e.matmul(bank[:, ti * 256:(ti + 1) * 256],
                                                        lhsT=xbf[:, k, tb * 128:(tb + 1) * 128], rhs=sl[:, k, :],
                                                        start=(k == 0), stop=(k == KC - 1))
                                return last
                            fw.op("pe", [SL] + XB, [BANK], f)
                            evac(cbk, tp, bank, BANK)
                        units.append(unit)
                return units

            def ev_u(cc, bank, BANK):
                fw.op("act", [BANK], [G[cc]], lambda e: e.activation(out=arena[:, cc, :], in_=bank[:], func=AF.Gelu_apprx_tanh))

            def ev_v(cbk, tp, bank, BANK):
                fw.op("act", [BANK], [R[4 * tp + 0], R[4 * tp + 1], R[4 * tp + 2], R[4 * tp + 3]],
                      lambda e: e.activation(out=vst[:, 2 * tp:2 * tp + 2, cbk * 256:(cbk + 1) * 256],
                                             in_=bank[:].rearrange("p (a c) -> p a c", c=256),
                                             func=AF.Gelu_apprx_tanh))

            def ev_q(cc, bank, BANK):
                fw.op("act", [BANK], [G[8 + cc]], lambda e: e.activation(out=arena[:, 8 + cc, :], in_=bank[:], func=AF.Silu))

            def ev_og(cc, bank, BANK):
                fw.op("act", [BANK], [G[16 + cc]], lambda e: e.activation(out=arena[:, 16 + cc, :], in_=bank[:], func=AF.Silu))

            def ev_i(cbk, tp, bank, BANK):
                fw.op("act", [BANK], [ITM[2 * tp], ITM[2 * tp + 1]],
                      lambda e: e.activation(out=itm[:, 2 * tp:2 * tp + 2, cbk * 256:(cbk + 1) * 256],
                                             in_=bank[:].rearrange("p (a c) -> p a c", c=256), func=AF.Copy))

            def ev_ga(cc, bank, BANK):
                fw.op("act", [BANK], [SGA[cc]], lambda e: e.activation(out=sga[:, cc, :], in_=bank[:], func=AF.Sigmoid))

            def ev_gb(cc, bank, BANK):
                fw.op("act", [BANK], [SGB[cc]], lambda e: e.activation(out=sgb[:, cc, :], in_=bank[:], func=AF.Sigmoid))

            yield tm_units(4, ev_i, MM)
            yield fm_units(2, ev_q, MM) + fm_units(5, ev_og, MM)
            prefetch_x(tt + 1)
            fw.dma("pool", s_p, [], [PB], lambda e: e.dma_start(out=pbf[:], in_=pTv[:, :, tsl]))

            def sgu_S0():
                for tb in range(4):
                    Rb = [R[2 * tb], R[2 * tb + 1]]
                    for hh in range(2):
                        fw.op("dve", Rb, [LNS[tb]], lambda e, hh=hh, tb=tb: e.bn_stats(out=bnst[:, tb, hh, :], in_=vst[:, tb, hh * 512:(hh + 1) * 512]))
                    fw.op("dve", [LNS[tb]], [LNS[tb]], lambda e, tb=tb: e.bn_aggr(out=mv[:, tb, :], in_=bnst[:, tb].rearrange("p a s -> p (a s)")))

            def sgu_S1():
                fw.op("act", LNS + [CONSTB], LNS, lambda e: e.activation(out=rsv[:], in_=mv[:, :, 1], func=AF.Ln, bias=epsln[:], scale=1.0))
                fw.op("act", LNS, LNS, lambda e: e.activation(out=rsv[:], in_=rsv[:], func=AF.Exp, scale=-0.5))
                fw.op("dve", LNS, LNS, lambda e: e.scalar_tensor_tensor(out=nmv[:], in0=mv[:, :, 0], scalar=-1.0, in1=rsv[:], op0=ALU.mult, op1=ALU.mult))

            def sgu_S2():
                for tb in range(4):
                    Rb = [R[2 * tb], R[2 * tb + 1]]
                    fw.op("act", Rb + [LNS[tb]], Rb, lambda e, tb=tb: e.activation(
                        out=vst[:, tb, :], in_=vst[:, tb, :], func=AF.Identity, bias=nmv[:, tb:tb + 1], scale=rsv[:, tb:tb + 1]))

            def sgu_S3(tbs):
                for tb in tbs:
                    Rb = [R[2 * tb], R[2 * tb + 1]]
                    fw.op("dve", Rb + [CST], Rb, lambda e, tb=tb: e.tensor_tensor(out=vst[:, tb, :], in0=vst[:, tb, :], in1=cs("gv", 0, 1024), op=ALU.mult))
                    fw.op("dve", Rb + [CST], [VN[tb]], lambda e, tb=tb: e.tensor_tensor(out=vn[:, tb, :], in0=vst[:, tb, :], in1=cs("bv", 0, 1024), op=ALU.add))

            def sgu_unit(tb):
                for gq in range(2):
                    bank, BANK = FMM.next()

                    def f(e, bank=bank, gq=gq):
                        last = None
                        for gi in range(4):
                            g = gq * 4 + gi
                            e.matmul(bank[:, gi * 128:(gi + 1) * 128], lhsT=vn[:, tb, g * 128:(g + 1) * 128],
                                     rhs=wsT[:, g, :], start=True, stop=False)
                            last = e.matmul(bank[:, gi * 128:(gi + 1) * 128], lhsT=ones[:, :],
                                            rhs=bsp[:, g * 128:(g + 1) * 128], start=False, stop=True)
                        return last
                    fw.op("pe", [VN[tb], WST, BSP, CONSTB], [BANK], f)
                    Gq = G[gq * 4:gq * 4 + 4]
                    fw.op("dve", [BANK] + Gq, Gq, lambda e, bank=bank, gq=gq: e.tensor_tensor(
                        out=arena[:, gq * 4:gq * 4 + 4, tb * 128:(tb + 1) * 128],
                        in0=bank[:].rearrange("p (a t) -> p a t", t=128),
                        in1=arena[:, gq * 4:gq * 4 + 4, tb * 128:(tb + 1) * 128], op=ALU.mult))

            nop = lambda: None

            def hgrn_E0(p):
                if p >= 4:
                    return
                sl, SL = wload(f"win3_{p}", win_v[:, :, 3072 + p * 256:3072 + p * 256 + 256], KC, 256, tt)
                for s in range(2):
                    xb_, XB_ = XBK[s]
                    fw.op("pe", [SL] + XB, [XB_], lambda e, xb_=xb_, s=s, sl=sl: mm8(e, xb_[:], sl, s, xbf))

            def hgrn_head(p, fill, tails, hooks):
                hs = (2 * p, 2 * p + 1)
                for s, h in enumerate(hs):
                    xb_, XB_ = XBK[s]
                    tA, TA = FT[s][0]
                    tB, TB = FT[s][1]
                    fw.op("act", [XB_], [TA], lambda e, tA=tA, xb_=xb_: e.activation(out=tA, in_=xb_[:], func=AF.Sigmoid))
                    fw.op("act", [XB_], [TB], lambda e, tB=tB, xb_=xb_: e.activation(out=tB, in_=xb_[:], func=AF.Sigmoid, scale=-1.0))
                hooks.get("E1", nop)()
                for s, h in enumerate(hs):
                    tA, TA = FT[s][0]
                    fw.op("act", [TA, LB], [TA], lambda e, tA=tA, h=h: e.activation(
                        out=tA, in_=tA, func=AF.Identity, bias=lbt[:, h:h + 1], scale=omlt[:, h:h + 1]))
                for s, h in enumerate(hs):
                    tA, TA = FT[s][0]
                    fw.op("act", [TA], [TA], lambda e, tA=tA: e.activation(out=tA, in_=tA, func=AF.Ln))
                for s, h in enumerate(hs):
                    tA, TA = FT[s][0]
                    tC, TC = FT[s][2]
                    fw.op("dve", [TA, CONSTB], [TC], lambda e, tA=tA, tC=tC: e.tensor_tensor_scan(
                        out=tC, data0=msk[:], data1=tA, initial=0.0, op0=ALU.mult, op1=ALU.add))
                hooks.get("E4", nop)()
                for s, h in enumerate(hs):
                    tA, TA = FT[s][0]
                    tC, TC = FT[s][2]
                    fw.op("act", [TC], [TA], lambda e, tA=tA, tC=tC: e.activation(out=tA, in_=tC, func=AF.Exp))
                    fw.op("act", [TC], [TC], lambda e, tC=tC: e.activation(out=tC, in_=tC, func=AF.Exp, scale=-1.0))
                for s, h in enumerate(hs):
                    tA, TA = FT[s][0]
                    tB, TB = FT[s][1]
                    tC, TC = FT[s][2]
                    kh, KH = KHTs[s]
                    fw.op("dve", [G[8 + h], TA], [G[8 + h]], lambda e, tA=tA, h=h: e.tensor_tensor(
                        out=arena[:, 8 + h, :], in0=arena[:, 8 + h, :], in1=tA, op=ALU.mult))
                    fw.op("dve", [TB, LB, TC], [TB], lambda e, tB=tB, tC=tC, h=h: e.scalar_tensor_tensor(
                        out=tB, in0=tB, scalar=omlt[:, h:h + 1], in1=tC, op0=ALU.mult, op1=ALU.mult))
                    fw.op("dve", [TA], [EL[h]], lambda e, tA=tA, h=h: e.tensor_copy(
                        out=el[:, h, :], in_=tA.rearrange("p (c s) -> p c s", s=64)[:, :, 63]))
                    fw.op("dve", [TB, EL[h]], [KH], lambda e, tB=tB, kh=kh, h=h: e.tensor_tensor(
                        out=kh.rearrange("p (c s) -> p c s", s=64), in0=tB.rearrange("p (c s) -> p c s", s=64),
                        in1=el[:, h, :].unsqueeze(2).to_broadcast([128, 8, 64]), op=ALU.mult))
                tails[0]()
                tails[1]()
                hooks.get("mid", nop)()
                for s, h in enumerate(hs):
                    tB, TB = FT[s][1]
                    kh, KH = KHTs[s]
                    pt, PT_ = PTRs[s]
                    fw.op("act", [TB], [KT[h]], lambda e, tB=tB, h=h: e.activation(out=ktT[:, h, :], in_=tB, func=AF.Copy))
                    fw.op("pe", [KH, CONSTB], [PT_], lambda e, kh=kh, pt=pt: [e.transpose(
                        out=pt[:, b * 128:(b + 1) * 128], in_=kh[:, b * 128:(b + 1) * 128], identity=ident[:]) for b in range(4)][-1])
                for s, h in enumerate(hs):
                    pt, PT_ = PTRs[s]
                    ka, kb, KAB = KHABs[s]
                    fw.op("act", [PT_], [KAB], lambda e, ka=ka, pt=pt: e.activation(
                        out=ka[0:64], in_=pt[0:64, :].rearrange("p (b c) -> p b c", c=128), func=AF.Copy))
                    fw.op("dve", [PT_, KAB], [KAB], lambda e, kb=kb, pt=pt: e.tensor_copy(
                        out=kb[64:128], in_=pt[64:128, :].rearrange("p (b c) -> p b c", c=128)))
                hooks.get("E7", nop)()
                fill[0]()
                fill[1]()
                hooks.get("E8", nop)()
                for s, h in enumerate(hs):
                    xb_, XB_ = XBK[s]
                    ka, kb, KAB = KHABs[s]
                    at, AT_ = ATTs[s]
                    sbh, SBH = SBFs[s]
                    fw.op("pe", [KT[h], G[8 + h]], [XB_], lambda e, xb_=xb_, h=h: [e.matmul(
                        xb_[:, b * 128:(b + 1) * 128], lhsT=ktT[:, h, b * 128:(b + 1) * 128],
                        rhs=arena[:, 8 + h, b * 128:(b + 1) * 128], start=True, stop=True) for b in range(4)][-1])
                    for half in range(2):
                        dbk, DBK = DBK2[half]

                        def f(e, dbk=dbk, half=half, ka=ka, kb=kb, h=h):
                            last = None
                            for ci in range(4):
                                c = half * 4 + ci
                                blk = c // 2
                                src = ka if (c % 2 == 0) else kb
                                last = e.matmul(dbk[:, ci * 128:(ci + 1) * 128], lhsT=src[:, blk, :],
                                                rhs=itm[:, blk, h * 128:(h + 1) * 128], start=True, stop=True)
                            return last
                        fw.op("pe", [KAB] + ITM, [DBK], f)
                    fw.op("dve", [XB_, CONSTB], [AT_], lambda e, at=at, xb_=xb_: e.tensor_tensor(
                        out=at, in0=xb_[:], in1=mask4[:].rearrange("p b t -> p (b t)"), op=ALU.mult))
                    fw.op("act", [S32[h], SBH], [SBH], lambda e, sbh=sbh, h=h: e.activation(out=sbh[:, 0, :], in_=s32[:, h, :], func=AF.Copy))
                    for c in range(8):
                        dbk, DBK = DBK2[c // 4]
                        ci = c % 4
                        rd = [SH, EL[h], DBK] + ([S32[h]] if c == 0 else [])
                        wr = [S32[h]] if c == 7 else [SH]

                        def frec(e, c=c, dbk=dbk, ci=ci, h=h):
                            src = s32[:, h, :] if c == 0 else sh[:, c, :]
                            dst = s32[:, h, :] if c == 7 else sh[:, c + 1, :]
                            return e.scalar_tensor_tensor(out=dst, in0=src, scalar=el[:, h, c:c + 1],
                                                          in1=dbk[:, ci * 128:(ci + 1) * 128], op0=ALU.mult, op1=ALU.add)
                        fw.op("dve", rd, wr, frec)
                    fw.op("act", [SH, SBH], [SBH], lambda e, sbh=sbh: e.activation(out=sbh[:, 1:8, :], in_=sh[:, 1:8, :], func=AF.Copy))
                    if s == 0:
                        fill[2]()
                        fill[3]()
                        fill[4]()
                fill[5]()
                hgrn_E0(p + 1)
                fill[6]()
                fill[7]()
                tails[2]()

            def hgrn_tail(p):
                hs = (2 * p, 2 * p + 1)

                def T0():
                  for s, h in enumerate(hs):
                    at, AT_ = ATTs[s]
                    sbh, SBH = SBFs[s]
                    ob, OB = OBK
                    tO, TO = FT[s][3]

                    def fo(e, at=at, sbh=sbh, h=h):
                        last = None
                        for b in range(4):
                            e.matmul(ob[:, b * 128:(b + 1) * 128], lhsT=itm[:, b, h * 128:(h + 1) * 128],
                                     rhs=at[:, b * 128:(b + 1) * 128], start=True, stop=False)
                            for hf in range(2):
                                c = 2 * b + hf
                                cs_ = slice(b * 128 + hf * 64, b * 128 + hf * 64 + 64)
                                last = e.matmul(ob[:, cs_], lhsT=sbh[:, c, :], rhs=arena[:, 8 + h, cs_],
                                                start=False, stop=(hf == 1))
                        return last
                    fw.op("pe", ITM + [AT_, SBH, G[8 + h]], [OB], fo)
                    sq, SQ = TBF.next()
                    fw.op("act", [OB], [SQ], lambda e, sq=sq: e.activation(out=sq, in_=ob[:], func=AF.Square))
                    fw.op("act", [OB, CST], [TO], lambda e, tO=tO, h=h: e.activation(
                        out=tO, in_=ob[:], func=AF.Identity, scale=cs("hg", h, 1)))
                    xb_, XB_ = XBK[s]
                    fw.op("pe", [SQ, CONSTB], [XB_], lambda e, xb_=xb_, sq=sq: e.matmul(xb_[:], lhsT=ones[:], rhs=sq, start=True, stop=True))

                def T1():
                  for s, h in enumerate(hs):
                    xb_, XB_ = XBK[s]
                    rs, RS = FT[s][4]
                    fw.op("act", [XB_, CONSTB], [RS], lambda e, rs=rs, xb_=xb_: e.activation(out=rs, in_=xb_[:], func=AF.Ln, bias=epsrms[:], scale=1.0 / 128.0))
                    fw.op("act", [RS], [RS], lambda e, rs=rs: e.activation(out=rs, in_=rs, func=AF.Exp, scale=-0.5))

                def T2():
                  for s, h in enumerate(hs):
                    tO, TO = FT[s][3]
                    rs, RS = FT[s][4]
                    fw.op("dve", [TO, RS], [RS], lambda e, rs=rs, tO=tO: e.tensor_tensor(out=rs, in0=tO, in1=rs, op=ALU.mult))
                    fw.op("dve", [RS, G[16 + h]], [KT[h]], lambda e, rs=rs, h=h: e.tensor_tensor(
                        out=ktT[:, h, :], in0=rs, in1=arena[:, 16 + h, :], op=ALU.mult))
                return (T0, T1, T2)

            fill_v = tm_units(1, ev_v, FMM)
            fill_u = fm_units(0, ev_u, FMM)
            fill_ga = fm_units(6, ev_ga, FMM)
            fill_gb = fm_units(7, ev_gb, FMM)

            fills = [fill_u, fill_v, fill_ga, fill_gb]
            hk = [{}, {},
                  {"E1": sgu_S0, "E4": sgu_S1, "mid": sgu_S2},
                  {"E1": lambda: sgu_S3((0, 1)), "E4": lambda: sgu_S3((2, 3)),
                   "E7": lambda: (sgu_unit(0), sgu_unit(1)), "E8": lambda: (sgu_unit(2), sgu_unit(3))}]
            tails = (nop, nop, nop)
            hgrn_E0(0)
            for p in range(4):
                hgrn_head(p, fills[p], tails, hk[p])
                tails = hgrn_tail(p)
            for t_ in tails:
                t_()

            for dp in range(4):
                c0 = dp * 256
                sa, SA = wload(f"wa{dp}", wa_v[:, :, c0:c0 + 256], KC, 256, tt)
                sbw, SBW = wload(f"wb{dp}", wb_v[:, :, c0:c0 + 256], KC, 256, tt)
                t1s = []
                for j in range(2):
                    dj = dp * 2 + j
                    ba, BA = MM.next()
                    fw.op("pe", [SA] + G[0:8], [BA], lambda e, ba=ba, sa=sa, j=j: mm8(e, ba[:], sa, j, arena[:, 0:8, :]))
                    t1, T1 = T32.next()
                    fw.op("dve", [BA, SGA[dj]], [T1], lambda e, t1=t1, ba=ba, dj=dj: e.tensor_tensor(out=t1, in0=ba[:], in1=sga[:, dj, :], op=ALU.mult))
                    t1s.append((t1, T1))
                for j in range(2):
                    dj = dp * 2 + j
                    t1, T1 = t1s[j]
                    bb, BB = MM.next()
                    fw.op("pe", [SBW] + KT, [BB], lambda e, bb=bb, sbw=sbw, j=j: mm8(e, bb[:], sbw, j, ktT))
                    t2, T2 = T32.next()
                    fw.op("dve", [BB, SGB[dj]], [T2], lambda e, t2=t2, bb=bb, dj=dj: e.tensor_tensor(out=t2, in0=bb[:], in1=sgb[:, dj, :], op=ALU.mult))
                    fw.op("dve", [T1, T2], [VN[dj // 2]], lambda e, t1=t1, t2=t2, dj=dj: e.tensor_tensor(out=mT[:, dj, :], in0=t1, in1=t2, op=ALU.add))

            fw.dma("pool", s_r, [], R, lambda e, tsl=tsl: e.dma_start(out=resid[:], in_=xTv[:, :, tsl]))

            def layer_norm_fm(gname, bname, write_bf):
                pass

            def stats_accum(dj, mb, MB, eb, EB):
                yb, YB = TBF.next()
                fw.op("act", [R[dj]], [YB], lambda e: e.activation(out=yb, in_=resid[:, dj, :], func=AF.Copy))
                ys, YS = TBF.next()
                fw.op("act", [R[dj]], [YS], lambda e: e.activation(out=ys, in_=resid[:, dj, :], func=AF.Square))

                def pe_part():
                    fw.op("pe", [YB, CONSTB], [MB], lambda e: e.matmul(mb[:], lhsT=onesD[:], rhs=yb, start=(dj == 0), stop=(dj == KC - 1)))
                    fw.op("pe", [YS, CONSTB], [EB], lambda e: e.matmul(eb[:], lhsT=onesD[:], rhs=ys, start=(dj == 0), stop=(dj == KC - 1)))
                return pe_part

            def normalize(gname, bname, rs, RS, nm, NM, bf_out):
                for dj in range(KC):
                    normalize_chunk(dj, gname, bname, rs, RS, nm, NM, bf_out)

            def normalize_chunk(dj, gname, bname, rs, RS, nm, NM, bf_out):
                if True:
                    fw.op("dve", [R[dj], RS], [R[dj]], lambda e, dj=dj: e.tensor_tensor(out=resid[:, dj, :], in0=resid[:, dj, :], in1=rs, op=ALU.mult))
                    fw.op(PLE, [R[dj], NM], [R[dj]], lambda e, dj=dj: e.tensor_tensor(out=resid[:, dj, :], in0=resid[:, dj, :], in1=nm, op=ALU.add))
                    if bf_out:
                        fw.op("act", [R[dj], CST], [XB[dj]], lambda e, dj=dj: e.activation(
                            out=xbf[:, dj, :], in_=resid[:, dj, :], func=AF.Identity,
                            bias=cs(bname, dj, 1), scale=cs(gname, dj, 1)))
                    fw.op("act", [R[dj], CST], [R[dj]], lambda e, dj=dj: e.activation(
                        out=resid[:, dj, :], in_=resid[:, dj, :], func=AF.Identity,
                        bias=cs(bname, dj, 1), scale=cs(gname, dj, 1)))


            mb, MB = AUX.next()
            eb, EB = AUX.next()
            pend = None
            for dp in range(4):
                c0 = dp * 256
                so, SO = wload(f"wo{dp}", wo_v[:, :, c0:c0 + 256], KC, 256, tt)
                for j in range(2):
                    dj = dp * 2 + j
                    bo, BO = MM.next()
                    fw.op("pe", [SO] + VN, [BO], lambda e, bo=bo, so=so, j=j: mm8(e, bo[:], so, j, mT))
                    if pend is not None:
                        pend()
                    fw.op("dve", [R[dj], BO], [R[dj]], lambda e, dj=dj, bo=bo: e.scalar_tensor_tensor(
                        out=resid[:, dj, :], in0=resid[:, dj, :], scalar=ALPHA, in1=bo[:], op0=ALU.mult, op1=ALU.add))
                    pend = stats_accum(dj, mb, MB, eb, EB)
            pend()
            yield
            st1 = {}

            def ln1_fin(mb=mb, MB=MB, eb=eb, EB=EB):
                st1["v"] = ln_finalize(mb, MB, eb, EB)

            def ln1_chunk(dj):
                rs, RS, nm, NM = st1["v"]
                normalize_chunk(dj, "ln1g", "ln1b", rs, RS, nm, NM, True)
            yield (ln1_fin, ln1_chunk)

            for fp in range(FC // 2):
                c0 = fp * 256
                sg_, SG_ = wload(f"wug{fp}", wup_v[:, :, c0:c0 + 256], KC, 256, tt)
                sv_, SV_ = wload(f"wuv{fp}", wup_v[:, :, DFF + c0:DFF + c0 + 256], KC, 256, tt)
                for j in range(2):
                    fc = fp * 2 + j
                    bg, BG = MM.next()
                    fw.op("pe", [SG_] + XB, [BG], lambda e, bg=bg, sg_=sg_, j=j: mm8(e, bg[:], sg_, j, xbf))
                    bv, BV = MM.next()
                    fw.op("pe", [SV_] + XB, [BV], lambda e, bv=bv, sv_=sv_, j=j: mm8(e, bv[:], sv_, j, xbf))
                    g_, GS_ = GS.next()
                    fw.op(PLE, [CY[fc], GS_], [GS_], lambda e, g_=g_, fc=fc: e.tensor_copy(out=g_[:, 0:2], in_=cy[:, fc, :]))
                    fw.op("act", [BG, GS_], [GS_], lambda e, g_=g_, bg=bg: e.activation(out=g_[:, 2:TT + 2], in_=bg[:], func=AF.Copy))
                    fw.op(PLE, [GS_, CY[fc]], [CY[fc]], lambda e, g_=g_, fc=fc: e.tensor_copy(out=cy[:, fc, :], in_=g_[:, TT:TT + 2]))
                    ac, AC = T32.next()
                    cw0 = _CST["cw"] + fc * 3
                    fw.op("act", [BG, CST], [AC], lambda e, ac=ac, bg=bg, cw0=cw0, fc=fc: e.activation(
                        out=ac, in_=bg[:], func=AF.Identity, bias=cs("cb", fc, 1), scale=cst[:, cw0 + 2:cw0 + 3]))
                    fw.op("dve", [GS_, CST, AC], [AC], lambda e, ac=ac, g_=g_, cw0=cw0: e.scalar_tensor_tensor(
                        out=ac, in0=g_[:, 1:TT + 1], scalar=cst[:, cw0 + 1:cw0 + 2], in1=ac, op0=ALU.mult, op1=ALU.add))
                    fw.op("dve", [GS_, CST, AC], [AC], lambda e, ac=ac, g_=g_, cw0=cw0: e.scalar_tensor_tensor(
                        out=ac, in0=g_[:, 0:TT], scalar=cst[:, cw0:cw0 + 1], in1=ac, op0=ALU.mult, op1=ALU.add))
                    fw.op("act", [AC], [AC], lambda e, ac=ac: e.activation(out=ac, in_=ac, func=AF.Gelu_apprx_tanh))
                    fw.op("dve", [AC, BV], [G[fc]], lambda e, ac=ac, bv=bv, fc=fc: e.tensor_tensor(
                        out=arena[:, fc, :], in0=bv[:], in1=ac, op=ALU.mult))

            mb, MB = AUX.next()
            eb, EB = AUX.next()
            pend = None
            for dp in range(4):
                sp_, SP_ = wload(f"wpg{dp}", wpg_v[:, :, dp * 256:dp * 256 + 256], KC, 256, tt)
                for j in range(2):
                    dj = dp * 2 + j
                    sd, SD = wload(f"wdn{dj}", wdn_v[:, :, dj * 128:dj * 128 + 128], FC, 128, tt)
                    bf_, BF_ = MM.next()
                    fw.op("pe", [SD] + G[0:FC], [BF_], lambda e, bf_=bf_, sd=sd: mm8(e, bf_[:], sd, 0, arena[:, 0:FC, :], nk=FC))
                    bpg, BPG = MM.next()
                    fw.op("pe", [SP_] + XB, [BPG], lambda e, bpg=bpg, sp_=sp_, j=j: mm8(e, bpg[:], sp_, j, xbf))
                    bpp, BPP = MM.next()
                    fw.op("pe", [WPP, PB], [BPP], lambda e, bpp=bpp, dj=dj: mm8(e, bpp[:], wpp[:, :, dj * 128:(dj + 1) * 128], 0, pbf, nk=2))
                    if pend is not None:
                        pend()
                    sgp, SGP = TBF.next()
                    fw.op("act", [BPG], [SGP], lambda e, sgp=sgp, bpg=bpg: e.activation(out=sgp, in_=bpg[:], func=AF.Sigmoid))
                    t1, T1 = T32.next()
                    fw.op("dve", [BPP, SGP], [T1], lambda e, t1=t1, bpp=bpp, sgp=sgp: e.tensor_tensor(out=t1, in0=bpp[:], in1=sgp, op=ALU.mult))
                    fw.op("dve", [R[dj], BF_], [R[dj]], lambda e, dj=dj, bf_=bf_: e.scalar_tensor_tensor(
                        out=resid[:, dj, :], in0=resid[:, dj, :], scalar=ALPHA, in1=bf_[:], op0=ALU.mult, op1=ALU.add))
                    fw.op(PLE, [R[dj], T1], [R[dj]], lambda e, dj=dj, t1=t1: e.tensor_tensor(
                        out=resid[:, dj, :], in0=resid[:, dj, :], in1=t1, op=ALU.add))
                    pend = stats_accum(dj, mb, MB, eb, EB)
            pend()
            yield
            st2 = {}

            def ln2_fin():
                st2["v"] = ln_finalize(mb, MB, eb, EB)

            def ln2_chunk(dj):
                rs, RS, nm, NM = st2["v"]
                normalize_chunk(dj, "ln2g", "ln2b", rs, RS, nm, NM, False)

            def ln2_out():
                out_toks.append(fw.dma("pool", s_o, R, [], lambda e: e.dma_start(out=outTv[:, :, tsl], in_=resid[:])))
            yield (ln2_fin, ln2_chunk, ln2_out)

        gens = [tile_gen(t_) for t_ in range(NT)]
        prefetch_x(0)
        for u_ in next(gens[0]):
            u_()
        for u_ in next(gens[0]):
            u_()
        for t_ in range(NT):
            g_t = gens[t_]
            next(g_t)
            units_i = list(next(gens[t_ + 1])) if t_ + 1 < NT else []
            fin1, chunk1 = next(g_t)
            fin1()
            for dj in range(KC):
                chunk1(dj)
                if dj < len(units_i):
                    units_i[dj]()
            next(g_t)
            units = list(next(gens[t_ + 1])) if t_ + 1 < NT else []
            fin, chunk, outd = next(g_t)
            fin()
            per = (len(units) + KC - 1) // KC if units else 0
            for dj in range(KC):
                chunk(dj)
                for u_ in units[dj * per:(dj + 1) * per]:
                    u_()
            outd()

        fw.final_wait("pool", out_toks[-1])

        block = es.enter_context(nc.Block())

        @block.sync
        def _(e):
            for th in fw.prog["sp"]:
                th(e)

        @block.tensor
        def _(e):
            for th in fw.prog["pe"]:
                th(e)

        @block.scalar
        def _(e):
            for th in fw.prog["act"]:
                th(e)

        @block.vector
        def _(e):
            for th in fw.prog["dve"]:
                th(e)

        @block.gpsimd
        def _(e):
            for th in fw.prog["pool"]:
                th(e)
    return nc


def _pk(v):
    return np.ascontiguousarray(np.asarray(v, np.float32).reshape(-1, 128).T)


def _prep_shared(inp):
    g = lambda k: np.asarray(inp[k], np.float32)
    cst = np.zeros((128, NCST), np.float32)

    def put(name, arr):
        cst[:, _CST[name]:_CST[name] + arr.shape[1]] = arr
    put("ln1g", _pk(g("ln1_g")[0]))
    put("ln1b", _pk(g("ln1_b")[0]))
    put("ln2g", _pk(g("ln2_g")[0]))
    put("ln2b", _pk(g("ln2_b")[0]))
    put("hg", _pk(g("hgrn_norm_g")[0]))
    put("l0", _pk(g("hgrn_lb_logits")[0]))
    put("l1", _pk(g("hgrn_lb_logits")[1]))
    cw = g("ffn_conv_w")[0]
    cwp = np.stack([_pk(cw[j]) for j in range(3)], axis=2)
    put("cw", cwp.reshape(128, 66))
    put("cb", _pk(g("ffn_conv_b")[0]))
    put("gv", np.broadcast_to(g("sgu_norm_g")[0][None, :], (128, 1024)))
    put("bv", np.broadcast_to(g("sgu_norm_b")[0][None, :], (128, 1024)))
    wsT = np.ascontiguousarray(g("sgu_w_s")[0].transpose(2, 0, 1)).reshape(128, 1024)
    return {
        "w_in": np.ascontiguousarray(g("w_in")[0]),
        "w_a": np.ascontiguousarray(g("w_branch")[0, 0]),
        "w_b": np.ascontiguousarray(g("w_branch")[0, 1]),
        "w_o": np.ascontiguousarray(g("w_out")[0]),
        "w_up": np.ascontiguousarray(g("ffn_w_up")[0]),
        "w_dn": np.ascontiguousarray(g("ffn_w_down")[0]),
        "w_pg": np.ascontiguousarray(g("ple_w_gate")[0]),
        "w_pp": np.ascontiguousarray(g("ple_w_proj")[0]),
        "wsT": wsT,
        "cst": cst,
        "bsb": np.ascontiguousarray(np.broadcast_to(g("sgu_b_s")[0].reshape(1, 1024), (128, 1024))),
    }


_NC_CACHE = {}


def kernel(**inputs):
    x = np.asarray(inputs["x"], np.float32)
    p = np.asarray(inputs["p"], np.float32)
    B, T, _ = x.shape
    assert B == NCORES and T % TT == 0
    shared = _prep_shared(inputs)
    in_maps = []
    for b in range(B):
        m = dict(shared)
        m["xT"] = np.ascontiguousarray(x[b].T)
        m["pT"] = np.ascontiguousarray(p[0, b].T)
        in_maps.append(m)
    if T not in _NC_CACHE:
        _NC_CACHE[T] = build(T)
    nc = _NC_CACHE[T]
    res = run_bass_kernel_spmd(nc, in_maps, core_ids=list(range(NCORES)))
    out = np.stack([np.ascontiguousarray(r["outT"].T) for r in res.results], axis=0)
    return out.astype(np.float32)
```
